# Optimizing a Trainium2 kernel written in Bass

```python
import math
import jax, jax.numpy as jnp
from jax import lax
import numpy as np

D_MODEL = 4096
BATCH = 4
SEQ = 4096
DEPTH = 1

N_HEADS_A = 16
HEAD_DIM_A = 128
N_IDX_HEADS = 32
IDX_DIM = 128
TOPK_MAX = 256
Q_BLOCK = 128
N_BUCKETS = 32
MAX_DISTANCE = 128
GMLP_WIDTH = 2048
N_GROUPS_B = 8
GROUP_DIM_B = GMLP_WIDTH // N_GROUPS_B
CHUNK = 128
D_FF = ((-(-8 * D_MODEL // 3) + 255) // 256) * 256
D_PLE = 256
EPS = 1e-6

A_WIDTH = N_HEADS_A * HEAD_DIM_A
SPLIT_SIZES = (A_WIDTH, HEAD_DIM_A, HEAD_DIM_A, N_IDX_HEADS * IDX_DIM, IDX_DIM, N_IDX_HEADS, 2 * GMLP_WIDTH, D_MODEL, D_MODEL)
IN_COLS = sum(SPLIT_SIZES)

kernel_name = "hybrid_dsa_gmlp_gated_block"


def rmsnorm(x, g):
    xf = x.astype(jnp.float32)
    y = xf * lax.rsqrt(jnp.mean(xf * xf, axis=-1, keepdims=True) + EPS)
    return (y * g.astype(jnp.float32)).astype(x.dtype)


def layernorm(x, g, b):
    xf = x.astype(jnp.float32)
    mu = jnp.mean(xf, axis=-1, keepdims=True)
    var = jnp.mean(jnp.square(xf - mu), axis=-1, keepdims=True)
    y = (xf - mu) * lax.rsqrt(var + EPS)
    return (y * g.astype(jnp.float32) + b.astype(jnp.float32)).astype(x.dtype)


def t5_bucket(n):
    max_exact = N_BUCKETS // 2
    n = jnp.maximum(n, 0)
    nf = jnp.maximum(n, 1).astype(jnp.float32)
    large = max_exact + (jnp.log(nf / max_exact) / math.log(MAX_DISTANCE / max_exact) * (N_BUCKETS - max_exact)).astype(jnp.int32)
    large = jnp.minimum(large, N_BUCKETS - 1)
    return jnp.where(n < max_exact, n, large)


def dsa_attention(q, k, v, q_idx, k_idx, w_idx, rel_bias):
    B, S = q.shape[0], q.shape[1]
    k_sel = min(TOPK_MAX, S // 4)
    n_blk = S // Q_BLOCK
    idx_scale = IDX_DIM ** -0.5 * N_IDX_HEADS ** -0.5
    attn_scale = HEAD_DIM_A ** -0.5
    key_pos = jnp.arange(S, dtype=jnp.int32)
    k_idx_f = k_idx.astype(jnp.float32)
    gather = jax.vmap(lambda t, i: t[i])

    def blk(a):
        return jnp.swapaxes(a.reshape((B, n_blk, Q_BLOCK) + a.shape[2:]), 0, 1)

    def one_block(args):
        j, qb, qib, wb = args
        q_pos = j * Q_BLOCK + jnp.arange(Q_BLOCK, dtype=jnp.int32)
        dots = jnp.einsum('bqhd,bsd->bqhs', qib.astype(jnp.float32), k_idx_f)
        score = jnp.einsum('bqhs,bqh->bqs', jax.nn.relu(dots), wb.astype(jnp.float32)) * idx_scale
        causal = key_pos[None, :] <= q_pos[:, None]
        score = jnp.where(causal[None], score, -jnp.inf)
        _, idx = lax.top_k(score, k_sel)
        ks = gather(k, idx)
        vs = gather(v, idx)
        valid = idx <= q_pos[None, :, None]
        bias = rel_bias[t5_bucket(q_pos[None, :, None] - idx)]
        logits = jnp.einsum('bqhd,bqkd->bhqk', qb, ks).astype(jnp.float32) * attn_scale
        logits = logits + jnp.transpose(bias, (0, 3, 1, 2)).astype(jnp.float32)
        logits = jnp.where(valid[:, None], logits, -jnp.inf)
        probs = jax.nn.softmax(logits, axis=-1).astype(vs.dtype)
        return jnp.einsum('bhqk,bqkd->bqhd', probs, vs)

    out = lax.map(one_block, (jnp.arange(n_blk, dtype=jnp.int32), blk(q), blk(q_idx), blk(w_idx)))
    return jnp.swapaxes(out, 0, 1).reshape(B, S, N_HEADS_A * HEAD_DIM_A)


def chunked_sgu(uv, ln_g, ln_b, w_s, b_s):
    B, S = uv.shape[0], uv.shape[1]
    uv = jax.nn.gelu(uv)
    u, v = uv[..., :GMLP_WIDTH], uv[..., GMLP_WIDTH:]
    v = layernorm(v, ln_g, ln_b)
    vg = v.reshape(B, S // CHUNK, CHUNK, N_GROUPS_B, GROUP_DIM_B)
    tril = jnp.tril(jnp.ones((CHUNK, CHUNK), dtype=bool))
    ws = jnp.where(tril[None], w_s, jnp.zeros_like(w_s))
    mixed = jnp.einsum('gts,bcsgd->bctgd', ws, vg) + jnp.transpose(b_s)[None, None, :, :, None]
    return u * mixed.reshape(B, S, GMLP_WIDTH)


def setup_inputs(seed: int = 0) -> dict:
    key = jax.random.key(seed)
    ks = jax.random.split(key, 24)
    nrm = lambda k, shape, s: jax.random.normal(k, shape, jnp.float32) * s
    L = DEPTH
    return {
        "x": nrm(ks[0], (BATCH, SEQ, D_MODEL), 1.0),
        "p": nrm(ks[1], (DEPTH, BATCH, SEQ, D_PLE), 1.0),
        "w_in": nrm(ks[2], (L, D_MODEL, IN_COLS), D_MODEL ** -0.5),
        "q_norm_g": 1.0 + nrm(ks[3], (L, HEAD_DIM_A), 0.02),
        "k_norm_g": 1.0 + nrm(ks[4], (L, HEAD_DIM_A), 0.02),
        "rel_bias": nrm(ks[5], (N_BUCKETS, N_HEADS_A), 0.5),
        "sgu_ln_g": 1.0 + nrm(ks[6], (L, GMLP_WIDTH), 0.02),
        "sgu_ln_b": nrm(ks[7], (L, GMLP_WIDTH), 0.02),
        "sgu_w": nrm(ks[8], (L, N_GROUPS_B, CHUNK, CHUNK), CHUNK ** -0.5),
        "sgu_b": 1.0 + nrm(ks[9], (L, N_GROUPS_B, CHUNK), 0.02),
        "w_branch_a": nrm(ks[10], (L, A_WIDTH, D_MODEL), A_WIDTH ** -0.5),
        "w_branch_b": nrm(ks[11], (L, GMLP_WIDTH, D_MODEL), GMLP_WIDTH ** -0.5),
        "w_out": nrm(ks[12], (L, D_MODEL, D_MODEL), D_MODEL ** -0.5),
        "norm_mix_g": 1.0 + nrm(ks[13], (L, D_MODEL), 0.02),
        "norm_ffn_g": 1.0 + nrm(ks[14], (L, D_MODEL), 0.02),
        "w_gate_ffn": nrm(ks[15], (L, D_MODEL, D_FF), D_MODEL ** -0.5),
        "w_up_ffn": nrm(ks[16], (L, D_MODEL, D_FF), D_MODEL ** -0.5),
        "w_down_ffn": nrm(ks[17], (L, D_FF, D_MODEL), D_FF ** -0.5),
        "w_ple": nrm(ks[18], (L, D_PLE, D_MODEL), D_PLE ** -0.5),
        "ple_norm_g": 1.0 + nrm(ks[19], (L, D_MODEL), 0.02),
        "w_ple_gate": nrm(ks[20], (L, D_MODEL, D_MODEL), D_MODEL ** -0.5),
        "norm_ple_g": 1.0 + nrm(ks[21], (L, D_MODEL), 0.02),
    }


def reference(x, p, w_in, q_norm_g, k_norm_g, rel_bias, sgu_ln_g, sgu_ln_b, sgu_w, sgu_b, w_branch_a, w_branch_b, w_out, norm_mix_g, norm_ffn_g, w_gate_ffn, w_up_ffn, w_down_ffn, w_ple, ple_norm_g, w_ple_gate, norm_ple_g):
    B, S = x.shape[0], x.shape[1]
    offsets = []
    acc = 0
    for sz in SPLIT_SIZES[:-1]:
        acc += sz
        offsets.append(acc)
    for i in range(DEPTH):
        h = rmsnorm(x, norm_mix_g[i])
        proj = h @ w_in[i]
        q, k, v, q_idx, k_idx, w_idx, uv, g_a, g_b = jnp.split(proj, offsets, axis=-1)
        q = rmsnorm(q.reshape(B, S, N_HEADS_A, HEAD_DIM_A), q_norm_g[i])
        k = rmsnorm(k, k_norm_g[i])
        q_idx = q_idx.reshape(B, S, N_IDX_HEADS, IDX_DIM)
        y_a = dsa_attention(q, k, v, q_idx, k_idx, w_idx, rel_bias)
        y_b = chunked_sgu(uv, sgu_ln_g[i], sgu_ln_b[i], sgu_w[i], sgu_b[i])
        merged = jax.nn.sigmoid(g_a) * (y_a @ w_branch_a[i]) + jax.nn.sigmoid(g_b) * (y_b @ w_branch_b[i])
        x = x + merged @ w_out[i]
        h2 = rmsnorm(x, norm_ffn_g[i])
        x = x + (jax.nn.silu(h2 @ w_gate_ffn[i]) * (h2 @ w_up_ffn[i])) @ w_down_ffn[i]
        pe = rmsnorm(p[i] @ w_ple[i], ple_norm_g[i])
        gate = jax.nn.sigmoid(rmsnorm(x, norm_ple_g[i]) @ w_ple_gate[i])
        x = x + gate * pe
    return x
```

```python
from contextlib import ExitStack
import math
import numpy as np
import concourse.bass as bass
import concourse.mybir as mybir
from concourse.bass_utils import run_bass_kernel_spmd

F32 = mybir.dt.float32
BF16 = mybir.dt.bfloat16
AF = mybir.ActivationFunctionType
ALU = mybir.AluOpType
AX = mybir.AxisListType

D = 4096
S = 4096
NH = 16
NIH = 32
DFF = 11008
GW = 2048
DPLE = 256
EPS = 1e-6
NEG = -30000.0
IDX_SCALE = (128 ** -0.5) * (32 ** -0.5)
ATT_SCALE = 128 ** -0.5
NBLK = 16
TT = 1024
NB = 8


def tile_spec(K, widths, ks_list):
    spec = []
    off = 0
    for w in widths:
        kts = []
        kc0 = 0
        for ks in ks_list:
            kts.append((off, kc0, ks))
            off += 128 * ks * w
            kc0 += ks
        assert kc0 * 128 == K
        spec.append((w, kts))
    return spec, off


def pack_w(W, widths, ks_list):
    K, N = W.shape
    spec, total = tile_spec(K, widths, ks_list)
    out = np.empty(total, np.float32)
    Wr = W.reshape(K // 128, 128, N)
    n0 = 0
    for (w, kts) in spec:
        for (off, kc0, ks) in kts:
            out[off:off + 128 * ks * w] = Wr[kc0:kc0 + ks, :, n0:n0 + w].transpose(1, 0, 2).reshape(-1)
        n0 += w
    return out


KS4096 = [8, 8, 8, 8]
KS2048 = [8, 8]
KS5504 = [8, 8, 8, 8, 8, 3]
KS256 = [2]
W_INP = [256] * 8 + [256] * 16 + [32] + [256] * 8 + [256] * 8 + [256] * 16 + [256] * 16
NT_Q, NT_QI, NT_WI, NT_V, NT_U, NT_GA, NT_GB = 0, 8, 24, 25, 33, 41, 57
W_KVI = [128, 128, 128]
W_4096 = [256] * 16
W_GU = [256] * 86


class Tk:
    __slots__ = ("w", "r", "name")

    def __init__(self, name=""):
        self.w = None
        self.r = {}
        self.name = name


class Sched:
    def __init__(self, nc, es):
        self.nc = nc
        self.es = es
        self.eng = {"pe": nc.tensor, "act": nc.scalar, "dve": nc.vector, "pool": nc.gpsimd, "sp": nc.sync}
        self.sem = {k: es.enter_context(nc.semaphore("s_" + k)) for k in ["pe", "act", "dve", "pool"]}
        self.cnt = {k: 0 for k in self.sem}
        self.waited = {k: {} for k in self.eng}
        self.dsems = []

    def new_dsem(self, name):
        sem = self.es.enter_context(self.nc.semaphore("d_" + name))
        d = {"sem": sem, "tot": 0, "key": "d_" + name}
        self.dsems.append(d)
        return d

    def _wait(self, e, key, sem, val):
        if val <= 0 or self.waited[e].get(key, 0) >= val:
            return
        self.eng[e].wait_ge(sem, val)
        self.waited[e][key] = val

    def _deps(self, e, reads, writes):
        deps = {}

        def add(tok, kind):
            key, sem, val, owner = tok
            if owner == e and (e == "pe" or kind == "war"):
                return
            if deps.get(key, (None, 0))[1] < val:
                deps[key] = (sem, val)

        for t in reads:
            if t.w:
                add(t.w, "raw")
        for t in writes:
            if t.w:
                add(t.w, "waw")
            for tok in t.r.values():
                add(tok, "war")
        for key, (sem, val) in deps.items():
            self._wait(e, key, sem, val)

    def op(self, e, fn, reads=(), writes=(), sig=True):
        self._deps(e, reads, writes)
        ins = fn(self.eng[e])
        if sig:
            self.cnt[e] += 1
            ins.then_inc(self.sem[e], 1)
            tok = (e, self.sem[e], self.cnt[e], e)
        else:
            tok = (e, self.sem[e], self.cnt[e] + 1, e)
        for t in reads:
            old = t.r.get(e)
            if old is None or old[2] < tok[2]:
                t.r[e] = tok
        for t in writes:
            t.w = tok
            t.r = {}
        return ins

    def dma(self, q, d, out, in_, reads=(), writes=(), **kw):
        self._deps(q, reads, writes)
        self._wait(q, d["key"], d["sem"], d["tot"])
        ins = self.eng[q].dma_start(out=out, in_=in_, **kw)
        d["tot"] += 16
        ins.then_inc(d["sem"], 16)
        tok = (d["key"], d["sem"], d["tot"], "dma")
        for t in reads:
            t.r[d["key"]] = tok
        for t in writes:
            t.w = tok
            t.r = {}

    def barrier(self):
        for e in self.eng:
            for k in self.sem:
                self._wait(e, k, self.sem[k], self.cnt[k])
            for d in self.dsems:
                self._wait(e, d["key"], d["sem"], d["tot"])


def build(debug=False, stop_after=99):
    nc = bass.Bass("TRN2", target_bir_lowering=False)
    es = ExitStack()
    sc = Sched(nc, es)
    op, dma = sc.op, sc.dma

    def din(name, shape, dt=F32):
        return nc.dram_tensor(name, list(shape), dt, kind="ExternalInput").ap()

    def dscr(name, shape, dt):
        return nc.dram_tensor(name, list(shape), dt, kind=("ExternalOutput" if debug else "Internal")).ap()

    sp_inp, n_inp = tile_spec(D, W_INP, KS4096)
    sp_kvi, n_kvi = tile_spec(D, W_KVI, KS4096)
    sp_wa, n_wa = tile_spec(2048, W_4096, KS2048)
    sp_wb, n_wb = tile_spec(2048, W_4096, KS2048)
    sp_wo, n_wo = tile_spec(D, W_4096, KS4096)
    sp_gu, n_gu = tile_spec(D, W_GU, KS4096)
    sp_dn, n_dn = tile_spec(5504, W_4096, KS5504)
    sp_pg, n_pg = tile_spec(D, W_4096, KS4096)
    sp_pl, n_pl = tile_spec(DPLE, W_4096, KS256)

    x_seq = din("x_seq", [32, 128, D])
    x_own = din("x_own", [NBLK, 128, D])
    p_own = din("p_own", [NBLK, 128, DPLE])
    w_inp = din("w_inp", [n_inp])
    w_kvi = din("w_kvi", [n_kvi])
    w_a = din("w_a", [n_wa])
    w_b = din("w_b", [n_wb])
    w_o = din("w_o", [n_wo])
    w_gu = din("w_gu", [n_gu])
    w_dn = din("w_dn", [2, n_dn])
    w_pg = din("w_pg", [n_pg])
    w_pl = din("w_pl", [n_pl])
    gcols_d = din("gcols", [128, 3, 32])
    gqk_d = din("gqk", [128, 2, 128])
    lnrep_d = din("lnrep", [128, 2, GW])
    pgrep_d = din("pgrep", [128, D])
    wsT_d = din("wsT", [128, 8, 128])
    trilT_d = din("trilT", [128, 128])
    bs_d = din("bs", [128, 8])
    relb_d = din("relb", [32, NH])
    emat_d = din("emat", [33, 512])
    cmask_d = din("cmask", [128, 256])
    ident_d = din("ident", [128, 512])
    anti_d = din("anti", [128, 128])
    out_d = nc.dram_tensor("out", [NBLK, 128, D], F32, kind="ExternalOutput").ap()

    qT_d = dscr("qT", [2, NB, 128, NH, 128], BF16)
    qiT_d = dscr("qiT", [2, NB, 128, NIH, 128], BF16)
    ybT_d = dscr("ybT", [2, 16, 128, TT], BF16)
    sga_d = dscr("sga", [2, NB, 128, D], BF16)
    sgb_d = dscr("sgb", [2, NB, 128, D], BF16)
    x1_d = dscr("x1", [NBLK, 128, D], F32)
    x2_d = dscr("x2", [NBLK, 128, D], F32)
    actT_d = dscr("actT", [2, 86, 128, TT], BF16)
    tb_d = dscr("tbias", [NH, 512], F32)
    yaT_d = dscr("yaT", [2, 128, NH, TT], BF16) if debug else None
    kt_dbg = dscr("kt_dbg", [3, 128, S], BF16) if debug else None

    t_qT = [Tk() for _ in range(2)]
    t_qiT = [Tk() for _ in range(2)]
    t_ybT = [Tk() for _ in range(2)]
    t_sga = [Tk() for _ in range(2)]
    t_sgb = [Tk() for _ in range(2)]
    t_x1 = [[Tk() for _ in range(16)] for _ in range(2)]
    t_x2 = [[Tk() for _ in range(16)] for _ in range(2)]
    t_actT = [Tk() for _ in range(2)]
    t_tb = Tk()

    uid = [0]

    def sb(st, name, shape, dt):
        uid[0] += 1
        return st.enter_context(nc.sbuf_tensor("%s_%d" % (name, uid[0]), list(shape), dt))

    PGp = es.enter_context(nc.psum_tensor("PG", [128, 2048], F32))
    PTp = es.enter_context(nc.psum_tensor("PT", [128, 1024], F32))
    PMp = es.enter_context(nc.psum_tensor("PM", [128, 1024], F32))
    PG = PGp[:]
    PT = PTp[:]
    PM = PMp[:]
    PTb = PT.bitcast(BF16)
    t_PG = [Tk("pg%d" % i) for i in range(4)]
    t_PT = [Tk("pt%d" % i) for i in range(2)]
    t_PM = [Tk("pm%d" % i) for i in range(2)]

    IDF = sb(es, "IDF", [128, 128], F32)
    JREP = sb(es, "JREP", [128, 512], BF16)
    ONES = sb(es, "ONES", [128, 128], BF16)
    GCOLS = sb(es, "GCOLS", [128, 3, 32], F32)
    GQK = sb(es, "GQK", [128, 2, 128], F32)
    BS = sb(es, "BS", [128, 8], F32)
    WSM = sb(es, "WSM", [128, 8, 128], BF16)
    WB = sb(es, "WB", [128, 4, 8 * 256], BF16)
    KT = sb(es, "KT", [128, S], BF16)
    KIT = sb(es, "KIT", [128, S], BF16)
    VV = sb(es, "VV", [128, 32, 128], BF16)
    WABS = sb(es, "WABS", [128, NBLK, 32], F32)
    WSGN = sb(es, "WSGN", [128, NBLK, 32], F32)
    SM = sb(es, "SM", [128, 64], F32)
    t_const = Tk("const")
    t_WB = [Tk("wb%d" % i) for i in range(4)]
    d_WB = [sc.new_dsem("wb%d" % i) for i in range(4)]
    t_KV = Tk("kv")
    t_W = Tk("wabs")
    t_SM = Tk("sm")
    d_misc = sc.new_dsem("misc")
    d_ld = [sc.new_dsem("ld%d" % i) for i in range(4)]
    d_st = [sc.new_dsem("st%d" % i) for i in range(4)]
    IDB = JREP[:, 0:128]

    with ExitStack() as st:
        tmpf = sb(st, "c_tmpf", [128, 8, 128], F32)
        tmpi = sb(st, "c_tmpi", [128, 512], F32)
        tril = sb(st, "c_tril", [128, 128], F32)
        t_tmp = Tk()
        dma("sp", d_misc, IDF[:], ident_d[:, 0:128], writes=[t_const])
        dma("sp", d_misc, tmpi[:], ident_d[:, :], writes=[t_tmp])
        op("dve", lambda e: e.tensor_copy(out=JREP[:], in_=tmpi[:]), reads=[t_tmp], writes=[t_const])
        op("dve", lambda e: e.memset(ONES[:], 1.0), writes=[t_const])
        dma("sp", d_misc, GCOLS[:], gcols_d[:, :, :], writes=[t_const])
        dma("sp", d_misc, GQK[:], gqk_d[:, :, :], writes=[t_const])
        dma("sp", d_misc, BS[:], bs_d[:, :], writes=[t_const])
        dma("sp", d_misc, tmpf[:], wsT_d[:, :, :], writes=[t_tmp])
        dma("sp", d_misc, tril[:], trilT_d[:, :], writes=[t_tmp])
        op("dve", lambda e: e.tensor_tensor(out=WSM[:], in0=tmpf[:], in1=tril[:].unsqueeze(1).broadcast_to([128, 8, 128]),
                                            op=ALU.mult), reads=[t_tmp], writes=[t_const])
        relb = sb(st, "c_relb", [33, NH], F32)
        emat = sb(st, "c_emat", [33, 512], F32)
        tbs = sb(st, "c_tbs", [NH, 512], F32)
        op("dve", lambda e: e.memset(relb[:], NEG), writes=[t_tmp])
        dma("sp", d_misc, relb[0:32, :], relb_d[:, :], reads=[], writes=[t_tmp])
        dma("sp", d_misc, emat[:], emat_d[:, :], writes=[t_tmp])
        op("pe", lambda e: e.matmul(PG[0:NH, 0:512], lhsT=relb[:], rhs=emat[:], start=True, stop=True),
           reads=[t_tmp], writes=[t_PG[0]])
        op("act", lambda e: e.activation(out=tbs[:], in_=PG[0:NH, 0:512], func=AF.Copy), reads=[t_PG[0]], writes=[t_tmp])
        dma("sp", d_misc, tb_d[:, :], tbs[:], reads=[t_tmp], writes=[t_tb])
        sc.barrier()

    wslot = [0]

    def gemm_ntile(inT, t_in, KC, wflat, spec, nt, nblk=NB):
        width, kts = spec[nt]
        for (off, kc0, KS) in kts:
            s = wslot[0] % 4
            wslot[0] += 1
            src = wflat[off:off + 128 * KS * width].rearrange("(p k) -> p k", p=128)
            dma("pool", d_WB[s], WB[:, s, 0:KS * width], src, writes=[t_WB[s]], max_dma_last_dim=8192)
            if kc0 + KS == KC:
                order = [(ks, blk) for blk in range(nblk) for ks in range(KS)]
            else:
                order = [(ks, blk) for ks in range(KS) for blk in range(nblk)]
            for oi, (ks, blk) in enumerate(order):
                kc = kc0 + ks
                b, sub = blk // 2, blk % 2
                last_of_tile = (oi == len(order) - 1)
                last_k = (kc == KC - 1)
                o = PG[:, b * 512 + sub * 256: b * 512 + sub * 256 + width]
                l = inT[:, kc, blk * 128:(blk + 1) * 128]
                r = WB[:, s, ks * width:(ks + 1) * width]
                op("pe", lambda e, o=o, l=l, r=r, st_=(kc == 0 and sub == 0), sp_=last_k:
                   e.matmul(o, lhsT=l, rhs=r, start=st_, stop=sp_, skip_group_check=True),
                   reads=[t_in, t_WB[s]], writes=[t_PG[b]], sig=(last_of_tile or last_k))
        return width

    def bankv(b, width):
        return PG[:, b * 512:(b + 1) * 512].rearrange("p (s c) -> p s c", s=2)[:, :, 0:width]

    def transposes_bf16(srcs, t_src, ptbank):
        for j, s_ap in enumerate(srcs):
            o = PTb[:, ptbank * 1024 + j * 128: ptbank * 1024 + (j + 1) * 128]
            op("pe", lambda e, o=o, s_ap=s_ap: e.transpose(o, s_ap, IDB), reads=[t_src, t_const],
               writes=[t_PT[ptbank]], sig=(j == len(srcs) - 1))

    def rstd_from_ss(ssap, n, dim):
        op("act", lambda e: e.activation(out=ssap, in_=ssap, func=AF.Sqrt, scale=1.0 / dim, bias=EPSC[:, 0:1]),
           reads=[t_SM, t_const], writes=[t_SM])
        op("dve", lambda e: e.reciprocal(out=ssap, in_=ssap), reads=[t_SM], writes=[t_SM])

    def rstd_act(ssap, dim):
        op("act", lambda e: e.activation(out=ssap, in_=ssap, func=AF.Ln, scale=1.0 / dim, bias=EPSC[:, 0:1]),
           reads=[t_SM, t_const], writes=[t_SM])
        op("act", lambda e: e.activation(out=ssap, in_=ssap, func=AF.Exp, scale=-0.5), reads=[t_SM], writes=[t_SM])

    EPSC = sb(es, "EPSC", [128, 1], F32)
    op("dve", lambda e: e.memset(EPSC[:], EPS), writes=[t_const])

    def norm_transpose(st_parent, srcs, gi, hT, t_hT):
        with ExitStack() as st:
            XB = sb(st, "nt_xb", [128, 2, D], F32)
            JK = sb(st, "nt_jk", [128, D], BF16)
            t_XB = [Tk(), Tk()]
            t_JK = Tk()
            def nt_front(bi):
                src, tsrc = srcs[bi]
                s = bi % 2
                dma("sp", d_ld[s], XB[:, s, :], src, reads=[tsrc], writes=[t_XB[s]])
                ssc = SM[:, s:s + 1]
                op("act", lambda e: e.activation(out=JK[:], in_=XB[:, s, :], func=AF.Square, accum_out=ssc),
                   reads=[t_XB[s]], writes=[t_JK, t_SM])
                rstd_act(ssc, D)
                op("pool", lambda e: e.tensor_scalar(out=XB[:, s, :], in0=XB[:, s, :], scalar1=ssc, scalar2=1.0,
                                                     op0=ALU.mult, op1=ALU.mult), reads=[t_XB[s], t_SM], writes=[t_XB[s]])

            def nt_back(bi):
                s = bi % 2
                for g4 in range(8):
                    pm = g4 % 2
                    for j in range(4):
                        kc = g4 * 4 + j
                        o = PM[:, pm * 512 + j * 128: pm * 512 + (j + 1) * 128]
                        i_ = XB[:, s, kc * 128:(kc + 1) * 128]
                        op("pe", lambda e, o=o, i_=i_: e.transpose(o, i_, IDF[:]), reads=[t_XB[s], t_const],
                           writes=[t_PM[pm]], sig=(j == 3))
                    o = hT[:, g4 * 4:(g4 + 1) * 4, bi * 128:(bi + 1) * 128]
                    i_ = PM[:, pm * 512:(pm + 1) * 512].rearrange("p (j c) -> p j c", j=4)
                    gb = GCOLS[:, gi, g4 * 4:(g4 + 1) * 4].unsqueeze(2).broadcast_to([128, 4, 128])
                    op("dve", lambda e, o=o, i_=i_, gb=gb: e.tensor_tensor(out=o, in0=i_, in1=gb, op=ALU.mult),
                       reads=[t_PM[pm], t_const], writes=[t_hT])

            nt_front(0)
            for bi in range(len(srcs)):
                if bi + 1 < len(srcs):
                    nt_front(bi + 1)
                nt_back(bi)
            sc.barrier()

    t_xseq = Tk()
    with ExitStack() as st:
        WKV = sb(st, "s0_wkv", [128, 32, 384], BF16)
        XB = sb(st, "s0_xb", [128, 3, D], F32)
        JK = sb(st, "s0_jk", [128, D], BF16)
        HB = sb(st, "s0_hb", [128, 2, 32, 128], BF16)
        KN = sb(st, "s0_kn", [128, 2, 2, 128], BF16)
        SQ = sb(st, "s0_sq", [128, 128], F32)
        t_WKV, t_JK, t_SQ = Tk(), Tk(), Tk()
        t_XB = [Tk(), Tk(), Tk()]
        t_HB = [Tk(), Tk()]
        t_KN = [Tk(), Tk()]
        for nt in range(3):
            for (off, kc0, KS) in sp_kvi[nt][1]:
                srcw = w_kvi[off:off + 128 * KS * 128].rearrange("(p k c) -> p k c", p=128, k=KS)
                dma("pool", d_WB[0], WKV[:, kc0:kc0 + KS, nt * 128:(nt + 1) * 128], srcw, writes=[t_WKV], max_dma_last_dim=8192)
        def s0_front(blk):
            s = blk % 3
            dma("sp", d_ld[s], XB[:, s, :], x_seq[blk], reads=[t_xseq], writes=[t_XB[s]])
            ssc = SM[:, s:s + 1]
            op("act", lambda e: e.activation(out=JK[:], in_=XB[:, s, :], func=AF.Square, accum_out=ssc),
               reads=[t_XB[s]], writes=[t_JK, t_SM])
            rstd_act(ssc, D)
            op("pool", lambda e: e.tensor_scalar(out=XB[:, s, :], in0=XB[:, s, :], scalar1=ssc, scalar2=1.0, op0=ALU.mult, op1=ALU.mult),
               reads=[t_XB[s], t_SM], writes=[t_XB[s]])

        def s0_back(blk):
            s = blk % 2
            x3 = blk % 3
            for g4 in range(8):
                pm = g4 % 2
                for j in range(4):
                    kc = g4 * 4 + j
                    o = PM[:, pm * 512 + j * 128: pm * 512 + (j + 1) * 128]
                    i_ = XB[:, x3, kc * 128:(kc + 1) * 128]
                    op("pe", lambda e: e.transpose(o, i_, IDF[:]), reads=[t_XB[x3], t_const], writes=[t_PM[pm]], sig=(j == 3))
                o = HB[:, s, g4 * 4:(g4 + 1) * 4, :]
                i_ = PM[:, pm * 512:(pm + 1) * 512].rearrange("p (j c) -> p j c", j=4)
                gb = GCOLS[:, 0, g4 * 4:(g4 + 1) * 4].unsqueeze(2).broadcast_to([128, 4, 128])
                op("dve", lambda e: e.tensor_tensor(out=o, in0=i_, in1=gb, op=ALU.mult), reads=[t_PM[pm], t_const], writes=[t_HB[s]])
            for kc in range(32):
                op("pe", lambda e: e.matmul(PG[:, s * 512: s * 512 + 384], lhsT=HB[:, s, kc, :], rhs=WKV[:, kc, :], start=(kc == 0), stop=(kc == 31)),
                   reads=[t_HB[s], t_WKV], writes=[t_PG[s]], sig=(kc == 31))

        def s0_epi(blk):
            s = blk % 2
            pgk = PG[:, s * 512: s * 512 + 128]
            pgv = PG[:, s * 512 + 128: s * 512 + 256]
            pgi = PG[:, s * 512 + 256: s * 512 + 384]
            ss2 = SM[:, 8 + s: 9 + s]
            op("act", lambda e: e.activation(out=SQ[:], in_=pgk, func=AF.Square, accum_out=ss2), reads=[t_PG[s]], writes=[t_SQ, t_SM])
            rstd_from_ss(ss2, 1, 128)
            op("dve", lambda e: e.scalar_tensor_tensor(out=KN[:, s, 0, :], in0=pgk, scalar=ss2, in1=GQK[:, 1, :], op0=ALU.mult, op1=ALU.mult),
               reads=[t_PG[s], t_SM, t_const], writes=[t_KN[s]])
            op("act", lambda e: e.activation(out=KN[:, s, 1, :], in_=pgi, func=AF.Copy), reads=[t_PG[s]], writes=[t_KN[s]])
            op("act", lambda e: e.activation(out=VV[:, blk, :], in_=pgv, func=AF.Copy), reads=[t_PG[s]], writes=[t_KV])
            for j in range(2):
                o = PTb[:, s * 1024 + j * 128: s * 1024 + (j + 1) * 128]
                op("pe", lambda e: e.transpose(o, KN[:, s, j, :], IDB), reads=[t_KN[s], t_const], writes=[t_PT[s]], sig=(j == 1))
            op("act", lambda e: e.activation(out=KT[:, blk * 128:(blk + 1) * 128], in_=PTb[:, s * 1024: s * 1024 + 128], func=AF.Copy),
               reads=[t_PT[s]], writes=[t_KV])
            op("act", lambda e: e.activation(out=KIT[:, blk * 128:(blk + 1) * 128], in_=PTb[:, s * 1024 + 128: s * 1024 + 256], func=AF.Copy),
               reads=[t_PT[s]], writes=[t_KV])

        s0_front(0)
        s0_front(1)
        s0_back(0)
        for blk in range(32):
            if blk + 2 < 32:
                s0_front(blk + 2)
            if blk + 1 < 32:
                s0_back(blk + 1)
            s0_epi(blk)
        sc.barrier()
        if debug:
            dma("sp", d_misc, kt_dbg[0], KT[:], reads=[t_KV])
            dma("sp", d_misc, kt_dbg[1], KIT[:], reads=[t_KV])
            dma("sp", d_misc, kt_dbg[2], VV[:].rearrange("p a b -> p (a b)"), reads=[t_KV])
        sc.barrier()

    t_xown = Tk()
    t_pown = Tk()
    t_out = Tk()
    sc_d = dscr("scores", [2, NB, 128, S], F32)
    ns_d = dscr("negsel", [2, NB, 128, S], BF16)
    t_scd = [[Tk() for _ in range(NB)] for _ in range(2)]
    t_nsd = [[Tk() for _ in range(NB)] for _ in range(2)]
    d_tk = sc.new_dsem("tk")
    d_sc = [sc.new_dsem("sc0"), sc.new_dsem("sc1")]
    state = {}
    t_dummy = Tk()
    DB = [0, 1, 3]

    def run(g):
        for _ in g:
            pass

    def s1(tt, pump=None):
        with ExitStack() as st:
            HT = sb(st, "s1_ht", [128, 32, TT], BF16)
            t_HT = Tk()
            norm_transpose(st, [(x_own[tt * 8 + bi], t_xown) for bi in range(8)], 0, HT, t_HT)
            VB = sb(st, "s1_vb", [128, NB, GW], BF16)
            t_VB = [Tk() for _ in range(NB)]
            STG = sb(st, "s1_stg", [128, 2, NB, 256], BF16)
            t_STG = [Tk(), Tk()]
            TS = sb(st, "s1_ts", [128, 2, 2, TT], BF16)
            t_TS = [Tk(), Tk()]
            SQ = sb(st, "s1_sq", [128, 4, 128], F32)
            t_SQ = Tk()
            LNR = sb(st, "s1_lnr", [128, 2, GW], F32)
            t_LNR = Tk()
            dma("sp", d_misc, LNR[:], lnrep_d[:, :, :], writes=[t_LNR])
            BNS = sb(st, "s1_bns", [128, 4, 6], F32)
            MV = sb(st, "s1_mv", [128, 2], F32)
            t_BN = Tk()
            GU = sb(st, "s1_gu", [128, 2, 2, 256], BF16)
            t_GU = [Tk(), Tk()]
            cnt = [0]

            def tm_to_dram_T(nt_local, dst5, t_dst, slot):
                for hh in range(2):
                    transposes_bf16([STG[:, slot, blk, hh * 128:(hh + 1) * 128] for blk in range(8)], t_STG[slot], hh)
                    op("act", lambda e, hh=hh: e.activation(out=TS[:, slot, hh, :], in_=PTb[:, hh * 1024:(hh + 1) * 1024],
                                                           func=AF.Copy), reads=[t_PT[hh]], writes=[t_TS[slot]])
                h0 = 2 * nt_local
                for hh in range(2):
                    dma("sp", d_st[slot], dst5[tt, :, :, h0 + hh, :].rearrange("b d q -> d b q"),
                        TS[:, slot, hh, :].rearrange("d (b q) -> d b q", b=NB), reads=[t_TS[slot]], writes=[t_dst])

            def sgu_mix():
                for blk in range(8):
                    for g in range(8):
                        b, sub = g // 2, g % 2
                        o = PG[:, b * 512 + sub * 256: b * 512 + (sub + 1) * 256]
                        op("pe", lambda e, o=o, g=g, blk=blk: e.matmul(o, lhsT=WSM[:, g, :], rhs=VB[:, blk, g * 256:(g + 1) * 256],
                                                                    start=True, stop=True, skip_group_check=True),
                           reads=[t_const, t_VB[blk]], writes=[t_PG[b]])
                    for g in range(8):
                        b, sub = g // 2, g % 2
                        i_ = PG[:, b * 512 + sub * 256: b * 512 + (sub + 1) * 256]
                        op("act", lambda e, i_=i_, g=g, blk=blk: e.activation(out=VB[:, blk, g * 256:(g + 1) * 256], in_=i_,
                                                                           func=AF.Identity, bias=BS[:, g:g + 1], scale=1.0),
                           reads=[t_PG[b], t_const], writes=[t_VB[blk]])

            S1_ORDER = list(range(0, NT_U)) + list(range(NT_GA, len(W_INP))) + list(range(NT_U, NT_GA))
            for nt in S1_ORDER:
                if pump is not None and nt > 0:
                    pump()
                width = gemm_ntile(HT, t_HT, 32, w_inp, sp_inp, nt)
                slot = cnt[0] % 2
                cnt[0] += 1
                if nt < NT_QI:
                    for b in range(4):
                        op("act", lambda e, b=b: e.activation(out=SQ[:].rearrange("p (s h) c -> p s (h c)", s=2),
                                                              in_=bankv(b, 256), func=AF.Square),
                           reads=[t_PG[b]], writes=[t_SQ])
                        op("dve", lambda e, b=b: e.tensor_reduce(out=SM[:, 16 + 4 * b: 20 + 4 * b], in_=SQ[:],
                                                                 axis=AX.X, op=ALU.add), reads=[t_SQ], writes=[t_SM])
                    rstd_from_ss(SM[:, 16:32], 16, 128)
                    for blk in range(8):
                        b, sub = blk // 2, blk % 2
                        for hh in range(2):
                            i_ = PG[:, b * 512 + sub * 256 + hh * 128: b * 512 + sub * 256 + (hh + 1) * 128]
                            c = 16 + 4 * b + 2 * sub + hh
                            op("dve", lambda e, blk=blk, hh=hh, i_=i_, c=c: e.scalar_tensor_tensor(
                                out=STG[:, slot, blk, hh * 128:(hh + 1) * 128], in0=i_, scalar=SM[:, c:c + 1],
                                in1=GQK[:, 0, :], op0=ALU.mult, op1=ALU.mult),
                               reads=[t_PG[b], t_SM, t_const], writes=[t_STG[slot]])
                    tm_to_dram_T(nt - NT_Q, qT_d, t_qT[tt], slot)
                elif nt < NT_WI:
                    for b in range(4):
                        op("act", lambda e, b=b: e.activation(out=STG[:, slot, 2 * b:2 * b + 2, :], in_=bankv(b, 256),
                                                              func=AF.Copy), reads=[t_PG[b]], writes=[t_STG[slot]])
                    tm_to_dram_T(nt - NT_QI, qiT_d, t_qiT[tt], slot)
                elif nt == NT_WI:
                    for b in range(4):
                        i0 = tt * 8 + 2 * b
                        op("act", lambda e, b=b, i0=i0: e.activation(out=WABS[:, i0:i0 + 2, :], in_=bankv(b, 32), func=AF.Abs,
                                                                     scale=IDX_SCALE), reads=[t_PG[b]], writes=[t_W])
                        op("act", lambda e, b=b, i0=i0: e.activation(out=WSGN[:, i0:i0 + 2, :], in_=bankv(b, 32), func=AF.Sign),
                           reads=[t_PG[b]], writes=[t_W])
                elif nt < NT_U:
                    c0 = (nt - NT_V) * 256
                    for b in range(4):
                        op("act", lambda e, b=b, c0=c0: e.activation(out=VB[:, 2 * b:2 * b + 2, c0:c0 + 256], in_=bankv(b, 256),
                                                                     func=AF.Gelu_apprx_tanh),
                           reads=[t_PG[b]], writes=[t_VB[2 * b], t_VB[2 * b + 1]])
                    if nt == NT_U - 1:
                        for blk in range(8):
                            for c4 in range(4):
                                op("dve", lambda e, c4=c4, blk=blk: e.bn_stats(out=BNS[:, c4, :], in_=VB[:, blk, c4 * 512:(c4 + 1) * 512]),
                                   reads=[t_VB[blk]], writes=[t_BN])
                            op("dve", lambda e: e.bn_aggr(out=MV[:], in_=BNS[:]), reads=[t_BN], writes=[t_BN])
                            op("act", lambda e: e.activation(out=SM[:, 32:33], in_=MV[:, 1:2], func=AF.Sqrt, scale=1.0, bias=EPSC[:, 0:1]),
                               reads=[t_BN, t_const], writes=[t_SM])
                            op("dve", lambda e: e.reciprocal(out=SM[:, 32:33], in_=SM[:, 32:33]), reads=[t_SM], writes=[t_SM])
                            op("dve", lambda e: e.scalar_tensor_tensor(out=SM[:, 33:34], in0=MV[:, 0:1], scalar=-1.0, in1=SM[:, 32:33],
                                                                       op0=ALU.mult, op1=ALU.mult), reads=[t_SM, t_BN], writes=[t_SM])
                            op("act", lambda e, blk=blk: e.activation(out=VB[:, blk, :], in_=VB[:, blk, :], func=AF.Identity,
                                                                      scale=SM[:, 32:33], bias=SM[:, 33:34]),
                               reads=[t_VB[blk], t_SM], writes=[t_VB[blk]])
                            op("dve", lambda e, blk=blk: e.tensor_tensor(out=VB[:, blk, :], in0=VB[:, blk, :], in1=LNR[:, 0, :], op=ALU.mult),
                               reads=[t_VB[blk], t_LNR], writes=[t_VB[blk]])
                            op("dve", lambda e, blk=blk: e.tensor_tensor(out=VB[:, blk, :], in0=VB[:, blk, :], in1=LNR[:, 1, :], op=ALU.add),
                               reads=[t_VB[blk], t_LNR], writes=[t_VB[blk]])
                elif nt < NT_GA:
                    c0 = (nt - NT_U) * 256
                    for b in range(4):
                        op("act", lambda e, b=b: e.activation(out=GU[:, b % 2, :, :], in_=bankv(b, 256), func=AF.Gelu_apprx_tanh),
                           reads=[t_PG[b]], writes=[t_GU[b % 2]])
                        op("dve", lambda e, b=b, c0=c0: e.tensor_tensor(out=STG[:, slot, 2 * b:2 * b + 2, :], in0=GU[:, b % 2, :, :],
                                                                        in1=VB[:, 2 * b:2 * b + 2, c0:c0 + 256], op=ALU.mult),
                           reads=[t_GU[b % 2], t_VB[2 * b], t_VB[2 * b + 1]], writes=[t_STG[slot]])
                    for hh in range(2):
                        transposes_bf16([STG[:, slot, blk, hh * 128:(hh + 1) * 128] for blk in range(8)], t_STG[slot], hh)
                        op("act", lambda e, hh=hh: e.activation(out=TS[:, slot, hh, :], in_=PTb[:, hh * 1024:(hh + 1) * 1024],
                                                               func=AF.Copy), reads=[t_PT[hh]], writes=[t_TS[slot]])
                    k0 = 2 * (nt - NT_U)
                    dma("sp", d_st[slot], ybT_d[tt, k0:k0 + 2, :, :].rearrange("k p t -> p k t"), TS[:, slot, :, :],
                        reads=[t_TS[slot]], writes=[t_ybT[tt]])
                else:
                    isb = nt >= NT_GB
                    c0 = (nt - (NT_GB if isb else NT_GA)) * 256
                    for b in range(4):
                        op("act", lambda e, b=b: e.activation(out=STG[:, slot, 2 * b:2 * b + 2, :], in_=bankv(b, 256),
                                                              func=AF.Sigmoid), reads=[t_PG[b]], writes=[t_STG[slot]])
                    dst = (sgb_d if isb else sga_d)[tt, :, :, c0:c0 + 256].rearrange("b p c -> p b c")
                    dma("sp", d_st[slot], dst, STG[:, slot, :, :], reads=[t_STG[slot]],
                        writes=[(t_sgb if isb else t_sga)[tt]])
                    if nt == NT_GB - 1:
                        sgu_mix()
            sc.barrier()

    def s3_idx(tt):
        with ExitStack() as st:
            QI = sb(st, "a_qi", [128, 1, NIH, 128], BF16)
            SCO = sb(st, "a_sc", [128, 2, S], F32)
            RR = sb(st, "a_rr", [128, 6, 512], BF16)
            CM = sb(st, "a_cm", [128, 256], F32)
            DBUF = [(PG[:, 0:512], t_PG[0]), (PG[:, 512:1024], t_PG[1]), (PG[:, 1536:2048], t_PG[3]),
                    (PT[:, 0:512], t_PT[0]), (PT[:, 512:1024], t_PT[1]), (PM[:, 0:512], t_PM[0])]
            DIAG = WB[:, 0:2, :].rearrange("p s (h c) -> p (s h) c", h=16)
            t_CM, t_DIAG = Tk(), Tk()
            t_QI = [Tk()]
            t_SC = [Tk(), Tk()]
            t_RR = [Tk() for _ in range(6)]
            dma("sp", d_misc, CM[:], cmask_d[:, :], writes=[t_CM])

            def idx_phase(il):
                i = tt * 8 + il
                par = il % 2
                N = (2 * i + 2) * 128
                dma("sp", d_ld[par], QI[:, 0], qiT_d[tt, il], reads=[t_qiT[tt]], writes=[t_QI[0]])
                op("pool", lambda e: e.tensor_tensor(out=DIAG, in0=IDB.unsqueeze(1).broadcast_to([128, NIH, 128]),
                                                     in1=WSGN[:, i, :].unsqueeze(2).broadcast_to([128, NIH, 128]), op=ALU.mult),
                   reads=[t_const, t_W], writes=[t_DIAG])
                for cc in range((N + 511) // 512):
                    c0 = cc * 512
                    w = min(512, N - c0)
                    LAG = 4
                    for h in range(NIH + LAG):
                        if h < NIH:
                            bk = h % 6
                            dap, dtk = DBUF[bk]
                            op("pe", lambda e, dap=dap, h=h: e.matmul(dap[:, 0:w], lhsT=QI[:, 0, h, :], rhs=KIT[:, c0:c0 + w],
                                                                    start=True, stop=True), reads=[t_QI[0], t_KV], writes=[dtk])
                            if h % 3 == 2:
                                op("dve", lambda e, bk=bk, dap=dap, h=h: e.tensor_scalar(
                                    out=RR[:, bk, 0:w], in0=dap[:, 0:w], scalar1=WABS[:, i, h:h + 1], scalar2=0.0,
                                    op0=ALU.mult, op1=ALU.max), reads=[dtk, t_W], writes=[t_RR[bk]])
                            else:
                                op("act", lambda e, bk=bk, dap=dap, h=h: e.activation(out=RR[:, bk, 0:w], in_=dap[:, 0:w], func=AF.Relu,
                                                                                   scale=WABS[:, i, h:h + 1]), reads=[dtk, t_W], writes=[t_RR[bk]])
                        if h >= LAG:
                            hp = h - LAG
                            op("pe", lambda e, hp=hp: e.matmul(PG[:, 1024:1024 + w], lhsT=DIAG[:, hp, :], rhs=RR[:, hp % 6, 0:w],
                                                               start=(hp == 0), stop=(hp == NIH - 1)),
                               reads=[t_DIAG, t_RR[hp % 6]], writes=[t_PG[2]])
                        yield
                    op("act", lambda e: e.activation(out=SCO[:, par, c0:c0 + w], in_=PG[:, 1024:1024 + w], func=AF.Copy),
                       reads=[t_PG[2]], writes=[t_SC[par]])


            for il in range(NB):
                N = (2 * (tt * 8 + il) + 2) * 128
                par = il % 2
                run(idx_phase(il))
                op("dve", lambda e: e.tensor_tensor(out=SCO[:, par, N - 256:N], in0=SCO[:, par, N - 256:N], in1=CM[:], op=ALU.add),
                   reads=[t_SC[par], t_CM], writes=[t_SC[par]])
                dma("sp", d_sc[par], sc_d[tt, il][:, 0:N], SCO[:, par, 0:N], reads=[t_SC[par]], writes=[t_scd[tt][il]])
            sc.barrier()

    def topk_gen(tt, SCO1, NS1, M8):
        t_S, t_N, t_M = Tk(), Tk(), Tk()
        for il in range(NB):
            N = (2 * (tt * 8 + il) + 2) * 128
            dma("sp", d_tk, SCO1[:, 0:N], sc_d[tt, il][:, 0:N], reads=[t_scd[tt][il]], writes=[t_S])
            yield
            for r8 in range(32):
                op("dve", lambda e, r8=r8: e.max(out=M8[:, r8 * 8:(r8 + 1) * 8], in_=SCO1[:, 0:N]), reads=[t_S], writes=[t_M])
                yield
                if r8 < 31:
                    op("dve", lambda e, r8=r8: e.match_replace(out=SCO1[:, 0:N], in_to_replace=M8[:, r8 * 8:(r8 + 1) * 8],
                                                               in_values=SCO1[:, 0:N], imm_value=-3.0e38),
                       reads=[t_S, t_M], writes=[t_S])
                    yield
            dma("sp", d_tk, SCO1[:, 0:N], sc_d[tt, il][:, 0:N], reads=[t_scd[tt][il]], writes=[t_S])
            op("dve", lambda e: e.tensor_scalar(out=NS1[:, 0:N], in0=SCO1[:, 0:N], scalar1=M8[:, 255:256], scalar2=NEG,
                                                op0=ALU.is_lt, op1=ALU.mult), reads=[t_S, t_M], writes=[t_N])
            yield
            dma("sp", d_tk, ns_d[tt, il][:, 0:N], NS1[:, 0:N], reads=[t_N], writes=[t_nsd[tt][il]])
            yield

    def s3_attn(tt):
        stt = ExitStack()
        YAT = sb(stt, "yat", [128, NH, TT], BF16)
        t_YAT = Tk()
        state[tt] = (stt, YAT, t_YAT)
        with ExitStack() as st:
            QQ = sb(st, "a_qq", [128, 3, NH, 128], BF16)
            NS = sb(st, "a_ns", [128, 2, S], BF16)
            BTH = sb(st, "a_bth", [128, 3, NH * 128], BF16)
            BTL = sb(st, "a_btl", [128, 3, NH * 128], BF16)
            PTs = sb(st, "a_pt", [128, 3, 512], BF16)
            PVs = sb(st, "a_pvs", [128, 2, 512], F32)
            LNs = sb(st, "a_lns", [128, 2, 512], F32)
            BTF = sb(st, "a_btf", [128, 512], F32)
            ANTI = sb(st, "a_anti", [128, 128], F32)
            t_PVs = [Tk(), Tk()]
            t_LNs = [Tk(), Tk()]
            t_BTF, t_BT, t_CM, t_TMPB = Tk(), Tk(), Tk(), Tk()
            t_QQ = [Tk(), Tk(), Tk()]
            t_NS = [Tk(), Tk()]
            t_PTs = [Tk(), Tk(), Tk()]
            dma("sp", d_misc, ANTI[:], anti_d[:, :], writes=[t_CM])

            with ExitStack() as st_b:
                TMPB = WB[:, 2:4, :].bitcast(F32).rearrange("p s (h c) -> p (s h) c", h=8)
                for m in range(3):
                    srcb = bass.AP(tensor=tb_d.tensor, offset=m * 128, ap=[[1, 128], [512, NH], [1, 128]])
                    dma("sp", d_misc, TMPB, srcb, reads=[t_tb], writes=[t_TMPB])
                    for j in range(4):
                        op("pe", lambda e, j=j: e.matmul(PG[:, j * 512:(j + 1) * 512], lhsT=ANTI[:], rhs=TMPB[:, 4 * j:4 * j + 4, :],
                                                          start=True, stop=True), reads=[t_CM, t_TMPB], writes=[t_PG[j]])
                        op("act", lambda e, j=j: e.activation(out=BTF[:], in_=PG[:, j * 512:(j + 1) * 512], func=AF.Copy, scale=1.0 / ATT_SCALE),
                           reads=[t_PG[j]], writes=[t_BTF])
                        op("dve", lambda e, j=j, m=m: e.tensor_copy(out=BTH[:, m, j * 512:(j + 1) * 512], in_=BTF[:]), reads=[t_BTF], writes=[t_BT])
                        op("dve", lambda e, j=j, m=m: e.tensor_tensor(out=BTL[:, m, j * 512:(j + 1) * 512], in0=BTF[:],
                                                                      in1=BTH[:, m, j * 512:(j + 1) * 512], op=ALU.subtract),
                           reads=[t_BTF, t_BT], writes=[t_BT])
                sc.barrier()

            def attn_phase(il):
                i = tt * 8 + il
                par = il % 2
                NCH = 2 * i + 2
                units = [(hg, c) for hg in range(4) for c in range(NCH)]

                def emit_L(k):
                    hg, c = units[k]
                    lb = k % 3
                    o = PG[:, lb * 512:(lb + 1) * 512]
                    op("pe", lambda e: e.matmul(o, lhsT=KT[:, c * 128:(c + 1) * 128], rhs=QQ[:, il % 3, 4 * hg:4 * hg + 4, :],
                                                start=True, stop=False),
                       reads=[t_KV, t_QQ[il % 3]], writes=[t_PG[lb]], sig=False)
                    m = NCH - 1 - c
                    near = m < 3
                    op("pe", lambda e: e.matmul(o, lhsT=NS[:, par, c * 128:(c + 1) * 128], rhs=JREP[:], start=False, stop=(not near)),
                       reads=[t_NS[par], t_const], writes=[t_PG[lb]], sig=(not near))
                    if near:
                        op("pe", lambda e: e.matmul(o, lhsT=IDB, rhs=BTH[:, m, hg * 512:(hg + 1) * 512], start=False, stop=False),
                           reads=[t_BT, t_const], writes=[t_PG[lb]], sig=False)
                        op("pe", lambda e: e.matmul(o, lhsT=IDB, rhs=BTL[:, m, hg * 512:(hg + 1) * 512], start=False, stop=True),
                           reads=[t_BT, t_const], writes=[t_PG[lb]])

                def emit_EV(k):
                    hg, c = units[k]
                    lb = k % 3
                    o = PG[:, lb * 512:(lb + 1) * 512]
                    op("act", lambda e: e.activation(out=PTs[:, lb, :], in_=o, func=AF.Exp, scale=ATT_SCALE),
                       reads=[t_PG[lb]], writes=[t_PTs[lb]])
                    op("pe", lambda e: e.matmul(PM[:, 0:512], lhsT=VV[:, c, :], rhs=PTs[:, lb, :], start=(c == 0), stop=(c == NCH - 1)),
                       reads=[t_KV, t_PTs[lb]], writes=[t_PM[0]], sig=False)
                    op("pe", lambda e: e.matmul(PM[:, 512:1024], lhsT=ONES[:], rhs=PTs[:, lb, :], start=(c == 0), stop=(c == NCH - 1)),
                       reads=[t_const, t_PTs[lb]], writes=[t_PM[1]])
                    if c == NCH - 1:
                        p2 = hg % 2
                        op("act", lambda e: e.activation(out=LNs[:, p2, :], in_=PM[:, 512:1024], func=AF.Ln), reads=[t_PM[1]], writes=[t_LNs[p2]])
                        op("act", lambda e: e.activation(out=LNs[:, p2, :], in_=LNs[:, p2, :], func=AF.Exp, scale=-1.0),
                           reads=[t_LNs[p2]], writes=[t_LNs[p2]])
                        op("act", lambda e: e.activation(out=PVs[:, p2, :], in_=PM[:, 0:512], func=AF.Copy), reads=[t_PM[0]], writes=[t_PVs[p2]])
                        op("pool", lambda e: e.tensor_tensor(
                            out=YAT[:, hg * 4:(hg + 1) * 4, il * 128:(il + 1) * 128], in0=PVs[:, p2, :].rearrange("p (h q) -> p h q", h=4),
                            in1=LNs[:, p2, :].rearrange("p (h q) -> p h q", h=4), op=ALU.mult),
                           reads=[t_PVs[p2], t_LNs[p2]], writes=[t_YAT])

                emit_L(0)
                emit_L(1)
                for k in range(len(units)):
                    if k + 2 < len(units):
                        emit_L(k + 2)
                    emit_EV(k)
                    yield

            def preload(il):
                N = (2 * (tt * 8 + il) + 2) * 128
                dma("sp", d_ld[2 + il % 2], QQ[:, il % 3], qT_d[tt, il], reads=[t_qT[tt]], writes=[t_QQ[il % 3]])
                dma("sp", d_ld[il % 2], NS[:, il % 2, 0:N], ns_d[tt, il][:, 0:N], reads=[t_nsd[tt][il]], writes=[t_NS[il % 2]])

            preload(0)
            for il in range(NB):
                if il + 1 < NB:
                    preload(il + 1)
                run(attn_phase(il))
            if debug:
                dma("sp", d_misc, yaT_d[tt], YAT[:], reads=[t_YAT])
            sc.barrier()

    def s45(tt):
        stt, YAT, t_YAT = state[tt]
        with ExitStack() as st:
            MT = sb(st, "b_mt", [128, 32, TT], BF16)
            st4 = ExitStack()
            YBT = sb(st4, "b_ybt", [128, 16, TT], BF16)
            TMPM = sb(st4, "b_tmpm", [128, NB, 256], F32)
            TMP2 = sb(st4, "b_tmp2", [128, 2, 2, 256], F32)
            SGT = sb(st4, "b_sgt", [128, 2, NB, 256], BF16)
            MB = sb(st4, "b_mb", [128, NB, 256], BF16)
            t_YBT, t_MT, t_TMPM, t_MB = Tk(), Tk(), Tk(), Tk()
            t_TMP2 = [Tk(), Tk()]
            t_SGT = [Tk(), Tk()]
            t_XT = [Tk(), Tk()]
            t_OT = [Tk(), Tk()]
            dma("sp", d_misc, YBT[:], ybT_d[tt].rearrange("k p t -> p k t"), reads=[t_ybT[tt]], writes=[t_YBT])
            for nt in range(16):
                c0 = nt * 256
                dma("sp", d_ld[0], SGT[:, 0], sga_d[tt, :, :, c0:c0 + 256].rearrange("b p c -> p b c"), reads=[t_sga[tt]], writes=[t_SGT[0]])
                dma("sp", d_ld[1], SGT[:, 1], sgb_d[tt, :, :, c0:c0 + 256].rearrange("b p c -> p b c"), reads=[t_sgb[tt]], writes=[t_SGT[1]])
                gemm_ntile(YAT, t_YAT, 16, w_a, sp_wa, nt)
                for b in range(4):
                    op("dve", lambda e, b=b: e.tensor_tensor(out=TMPM[:, 2 * b:2 * b + 2, :], in0=bankv(b, 256), in1=SGT[:, 0, 2 * b:2 * b + 2, :],
                                                             op=ALU.mult), reads=[t_PG[b], t_SGT[0]], writes=[t_TMPM])
                gemm_ntile(YBT, t_YBT, 16, w_b, sp_wb, nt)
                for b in range(4):
                    op("dve", lambda e, b=b: e.tensor_tensor(out=TMP2[:, b % 2], in0=bankv(b, 256), in1=SGT[:, 1, 2 * b:2 * b + 2, :],
                                                             op=ALU.mult), reads=[t_PG[b], t_SGT[1]], writes=[t_TMP2[b % 2]])
                    op("dve", lambda e, b=b: e.tensor_tensor(out=MB[:, 2 * b:2 * b + 2, :], in0=TMP2[:, b % 2], in1=TMPM[:, 2 * b:2 * b + 2, :],
                                                             op=ALU.add), reads=[t_TMP2[b % 2], t_TMPM], writes=[t_MB])
                for hh in range(2):
                    transposes_bf16([MB[:, blk, hh * 128:(hh + 1) * 128] for blk in range(8)], t_MB, hh)
                    op("act", lambda e, hh=hh, nt=nt: e.activation(out=MT[:, 2 * nt + hh, :], in_=PTb[:, hh * 1024:(hh + 1) * 1024], func=AF.Copy),
                       reads=[t_PT[hh]], writes=[t_MT])
            sc.barrier()
            st4.close()
            XT = sb(st, "b_xt", [128, 2, NB, 256], F32)
            OT = sb(st, "b_ot", [128, 2, NB, 256], F32)
            for nt in range(16):
                c0 = nt * 256
                s = nt % 2
                dma("sp", d_ld[2 + s], XT[:, s], x_own[tt * 8:(tt + 1) * 8, :, c0:c0 + 256].rearrange("b p c -> p b c"),
                    reads=[t_xown], writes=[t_XT[s]])
                gemm_ntile(MT, t_MT, 32, w_o, sp_wo, nt)
                for b in range(4):
                    op("dve", lambda e, b=b, s=s: e.tensor_tensor(out=OT[:, s, 2 * b:2 * b + 2, :], in0=bankv(b, 256), in1=XT[:, s, 2 * b:2 * b + 2, :],
                                                                  op=ALU.add), reads=[t_PG[b], t_XT[s]], writes=[t_OT[s]])
                dma("sp", d_st[s], x1_d[tt * 8:(tt + 1) * 8, :, c0:c0 + 256].rearrange("b p c -> p b c"), OT[:, s],
                    reads=[t_OT[s]], writes=[t_x1[tt][nt]])
            sc.barrier()
        stt.close()

    def s6(tt, pump=None):
        with ExitStack() as st:
            H2T = sb(st, "f_h2t", [128, 32, TT], BF16)
            t_H2T = Tk()
            norm_transpose(st, [(x1_d[tt * 8 + bi], t_dummy) for bi in range(8)], 1, H2T, t_H2T)
            AM = sb(st, "f_am", [128, 2, NB, 128], BF16)
            SL = sb(st, "f_sl", [128, 2, 2, 128], F32)
            AT = sb(st, "f_at", [128, 2, TT], BF16)
            t_AM = [Tk(), Tk()]
            t_SL = [Tk(), Tk()]
            t_AT = [Tk(), Tk()]
            for nt in range(86):
                if pump is not None and nt > 0:
                    pump()
                s = nt % 2
                gemm_ntile(H2T, t_H2T, 32, w_gu, sp_gu, nt)
                for b in range(4):
                    pv = bankv(b, 256)
                    op("act", lambda e, b=b, pv=pv: e.activation(out=SL[:, b % 2], in_=pv[:, :, 0:128], func=AF.Silu),
                       reads=[t_PG[b]], writes=[t_SL[b % 2]])
                    op("dve", lambda e, b=b, pv=pv, s=s: e.tensor_tensor(out=AM[:, s, 2 * b:2 * b + 2, :], in0=SL[:, b % 2], in1=pv[:, :, 128:256],
                                                                         op=ALU.mult), reads=[t_SL[b % 2], t_PG[b]], writes=[t_AM[s]])
                transposes_bf16([AM[:, s, blk, :] for blk in range(8)], t_AM[s], s)
                op("act", lambda e, s=s: e.activation(out=AT[:, s, :], in_=PTb[:, s * 1024:(s + 1) * 1024], func=AF.Copy),
                   reads=[t_PT[s]], writes=[t_AT[s]])
                dma("sp", d_st[s], actT_d[tt, nt], AT[:, s, :], reads=[t_AT[s]], writes=[t_actT[tt]])
            sc.barrier()
        with ExitStack() as st:
            ACTT = sb(st, "f_actt", [128, 43, TT], BF16)
            XT = sb(st, "f_xt", [128, 2, NB, 256], F32)
            OT = sb(st, "f_ot", [128, 2, NB, 256], F32)
            t_ACTT = Tk()
            t_XT = [Tk(), Tk()]
            t_OT = [Tk(), Tk()]
            for half in range(2):
                dma("sp", d_misc, ACTT[:], actT_d[tt, half * 43:(half + 1) * 43].rearrange("k p t -> p k t"),
                    reads=[t_actT[tt]], writes=[t_ACTT])
                for nt in range(16):
                    c0 = nt * 256
                    s = nt % 2
                    srcd = x1_d if half == 0 else x2_d
                    tsrc = t_x1[tt][nt] if half == 0 else t_x2[tt][nt]
                    dma("sp", d_ld[2 + s], XT[:, s], srcd[tt * 8:(tt + 1) * 8, :, c0:c0 + 256].rearrange("b p c -> p b c"),
                        reads=[tsrc], writes=[t_XT[s]])
                    if pump is not None:
                        pump()
                    gemm_ntile(ACTT, t_ACTT, 43, w_dn[half], sp_dn, nt)
                    for b in range(4):
                        op("dve", lambda e, b=b, s=s: e.tensor_tensor(out=OT[:, s, 2 * b:2 * b + 2, :], in0=bankv(b, 256),
                                                                      in1=XT[:, s, 2 * b:2 * b + 2, :], op=ALU.add),
                           reads=[t_PG[b], t_XT[s]], writes=[t_OT[s]])
                    dma("sp", d_st[s], x2_d[tt * 8:(tt + 1) * 8, :, c0:c0 + 256].rearrange("b p c -> p b c"), OT[:, s],
                        reads=[t_OT[s]], writes=[t_x2[tt][nt]])
            sc.barrier()

    def s7(tt):
        with ExitStack() as st:
            H3T = sb(st, "p_h3t", [128, 32, TT], BF16)
            t_H3T = Tk()
            norm_transpose(st, [(x2_d[tt * 8 + bi], t_dummy) for bi in range(8)], 2, H3T, t_H3T)
            PTT = sb(st, "p_ptt", [128, 2, TT], BF16)
            PB = sb(st, "p_pb", [128, 2, DPLE], F32)
            PGR = sb(st, "p_pgr", [128, D], F32)
            XT = sb(st, "p_xt", [128, 2, NB, 256], F32)
            OT = sb(st, "p_ot", [128, 2, NB, 256], F32)
            SGM = sb(st, "p_sgm", [128, NB, 256], F32)
            T1 = sb(st, "p_t1", [128, NB, 256], F32)
            SQ2 = sb(st, "p_sq2", [128, 2, 256], F32)
            t_PTT, t_PGR, t_SGM, t_T1, t_SQ2 = Tk(), Tk(), Tk(), Tk(), Tk()
            t_PB = [Tk(), Tk()]
            t_XT = [Tk(), Tk()]
            t_OT = [Tk(), Tk()]
            dma("sp", d_misc, PGR[:], pgrep_d[:, :], writes=[t_PGR])
            for bi in range(8):
                s = bi % 2
                dma("sp", d_ld[s], PB[:, s, :], p_own[tt * 8 + bi], reads=[t_pown], writes=[t_PB[s]])
                for j in range(2):
                    op("pe", lambda e, j=j, s=s: e.transpose(PM[:, s * 512 + j * 128: s * 512 + (j + 1) * 128], PB[:, s, j * 128:(j + 1) * 128], IDF[:]),
                       reads=[t_PB[s], t_const], writes=[t_PM[s]])
                op("act", lambda e, s=s, bi=bi: e.activation(out=PTT[:, :, bi * 128:(bi + 1) * 128],
                                                             in_=PM[:, s * 512: s * 512 + 256].rearrange("p (j c) -> p j c", j=2), func=AF.Copy),
                   reads=[t_PM[s]], writes=[t_PTT])
            op("dve", lambda e: e.memset(SM[:, 40:48], 0.0), writes=[t_SM])
            for nt in range(16):
                gemm_ntile(PTT, t_PTT, 2, w_pl, sp_pl, nt)
                for b in range(4):
                    op("act", lambda e, b=b: e.activation(out=SQ2[:], in_=bankv(b, 256), func=AF.Square), reads=[t_PG[b]], writes=[t_SQ2])
                    op("dve", lambda e, b=b: e.tensor_reduce(out=SM[:, 48:50], in_=SQ2[:], axis=AX.X, op=ALU.add), reads=[t_SQ2, t_SM], writes=[t_SM])
                    op("dve", lambda e, b=b: e.tensor_tensor(out=SM[:, 40 + 2 * b:42 + 2 * b], in0=SM[:, 40 + 2 * b:42 + 2 * b], in1=SM[:, 48:50],
                                                             op=ALU.add), reads=[t_SM], writes=[t_SM])
            rstd_from_ss(SM[:, 40:48], 8, D)
            for nt in range(16):
                c0 = nt * 256
                s = nt % 2
                dma("sp", d_ld[2 + s], XT[:, s], x2_d[tt * 8:(tt + 1) * 8, :, c0:c0 + 256].rearrange("b p c -> p b c"),
                    reads=[t_x2[tt][nt]], writes=[t_XT[s]])
                gemm_ntile(H3T, t_H3T, 32, w_pg, sp_pg, nt)
                for b in range(4):
                    op("act", lambda e, b=b: e.activation(out=SGM[:, 2 * b:2 * b + 2, :], in_=bankv(b, 256), func=AF.Sigmoid),
                       reads=[t_PG[b]], writes=[t_SGM])
                gemm_ntile(PTT, t_PTT, 2, w_pl, sp_pl, nt)
                for blk in range(8):
                    b, sub = blk // 2, blk % 2
                    i_ = PG[:, b * 512 + sub * 256: b * 512 + (sub + 1) * 256]
                    op("dve", lambda e, blk=blk, i_=i_, c0=c0: e.scalar_tensor_tensor(
                        out=T1[:, blk, :], in0=i_, scalar=SM[:, 40 + blk:41 + blk], in1=PGR[:, c0:c0 + 256], op0=ALU.mult, op1=ALU.mult),
                       reads=[t_PG[b], t_SM, t_PGR], writes=[t_T1])
                op("dve", lambda e: e.tensor_tensor(out=T1[:], in0=T1[:], in1=SGM[:], op=ALU.mult), reads=[t_T1, t_SGM], writes=[t_T1])
                op("dve", lambda e, s=s: e.tensor_tensor(out=OT[:, s], in0=T1[:], in1=XT[:, s], op=ALU.add),
                   reads=[t_T1, t_XT[s]], writes=[t_OT[s]])
                dma("sp", d_st[s], out_d[tt * 8:(tt + 1) * 8, :, c0:c0 + 256].rearrange("b p c -> p b c"), OT[:, s],
                    reads=[t_OT[s]], writes=[t_out])
            sc.barrier()

    def with_topk(tt_topk, stage_fn, tt_stage, per):
        with ExitStack() as stp:
            SCO1 = sb(stp, "k_sco", [128, S], F32)
            NS1 = sb(stp, "k_ns", [128, S], BF16)
            M8 = sb(stp, "k_m8", [128, 256], F32)
            g = topk_gen(tt_topk, SCO1, NS1, M8)

            def pump():
                for _ in range(per):
                    next(g, None)

            stage_fn(tt_stage, pump)
            run(g)
            sc.barrier()

    s1(0)
    s3_idx(0)
    with_topk(0, s1, 1, 8)
    s3_attn(0)
    s45(0)
    s3_idx(1)
    with_topk(1, s6, 0, 5)
    s7(0)
    s3_attn(1)
    s45(1)
    s6(1)
    s7(1)
    sc.barrier()
    return nc, sc, es


def t5_bucket_np(n):
    n = np.maximum(n, 0)
    nf = np.maximum(n, 1).astype(np.float32)
    large = 16 + (np.log(nf / 16) / math.log(128 / 16) * 16).astype(np.int32)
    large = np.minimum(large, 31)
    return np.where(n < 16, n, large)


def host_inputs(inp):
    f = lambda a: np.ascontiguousarray(np.asarray(a, dtype=np.float32))
    x = f(inp["x"]); p = f(inp["p"])[0]
    w_in = f(inp["w_in"])[0]
    offs = np.cumsum([0, 2048, 128, 128, 4096, 128, 32, 4096, 4096, 4096])
    q_, k_, v_, qi_, ki_, wi_, uv_, ga_, gb_ = [w_in[:, offs[i]:offs[i + 1]] for i in range(9)]
    w_own = np.concatenate([q_, qi_, wi_, uv_[:, GW:], uv_[:, :GW], ga_, gb_], axis=1)
    shared = {}
    shared["w_inp"] = pack_w(w_own, W_INP, KS4096)
    shared["w_kvi"] = pack_w(np.concatenate([k_, v_, ki_], axis=1), W_KVI, KS4096)
    shared["w_a"] = pack_w(f(inp["w_branch_a"])[0], W_4096, KS2048)
    shared["w_b"] = pack_w(f(inp["w_branch_b"])[0], W_4096, KS2048)
    shared["w_o"] = pack_w(f(inp["w_out"])[0], W_4096, KS4096)
    wg = f(inp["w_gate_ffn"])[0].reshape(D, 86, 1, 128); wu = f(inp["w_up_ffn"])[0].reshape(D, 86, 1, 128)
    shared["w_gu"] = pack_w(np.concatenate([wg, wu], axis=2).reshape(D, 2 * DFF), W_GU, KS4096)
    wd = f(inp["w_down_ffn"])[0]
    shared["w_dn"] = np.stack([pack_w(wd[:5504], W_4096, KS5504), pack_w(wd[5504:], W_4096, KS5504)])
    shared["w_pg"] = pack_w(f(inp["w_ple_gate"])[0], W_4096, KS4096)
    shared["w_pl"] = pack_w(f(inp["w_ple"])[0], W_4096, KS256)
    g3 = np.stack([f(inp["norm_mix_g"])[0], f(inp["norm_ffn_g"])[0], f(inp["norm_ple_g"])[0]])
    shared["gcols"] = np.ascontiguousarray(g3.reshape(3, 32, 128).transpose(2, 0, 1))
    shared["gqk"] = np.ascontiguousarray(np.broadcast_to(np.stack([f(inp["q_norm_g"])[0], f(inp["k_norm_g"])[0]])[None], (128, 2, 128)))
    shared["lnrep"] = np.ascontiguousarray(np.broadcast_to(np.stack([f(inp["sgu_ln_g"])[0], f(inp["sgu_ln_b"])[0]])[None], (128, 2, GW)))
    shared["pgrep"] = np.ascontiguousarray(np.broadcast_to(f(inp["ple_norm_g"])[0][None], (128, D)))
    shared["wsT"] = np.ascontiguousarray(f(inp["sgu_w"])[0].transpose(2, 0, 1))
    shared["trilT"] = np.ascontiguousarray(np.triu(np.ones((128, 128), np.float32)))
    shared["bs"] = np.ascontiguousarray(f(inp["sgu_b"])[0].T)
    shared["relb"] = f(inp["rel_bias"])
    shared["ident"] = np.ascontiguousarray(np.tile(np.eye(128, dtype=np.float32), (1, 4)))
    shared["anti"] = np.ascontiguousarray(np.fliplr(np.eye(128, dtype=np.float32)))
    maps = []
    for c in range(8):
        b, r = c // 2, c % 2
        m = dict(shared)
        xb = x[b].reshape(32, 128, D)
        m["x_seq"] = xb
        m["x_own"] = np.ascontiguousarray(xb[r::2])
        m["p_own"] = np.ascontiguousarray(p[b].reshape(32, 128, DPLE)[r::2])
        npr = np.arange(512)
        n = npr - 127 + (r - 1) * 128
        em = np.zeros((33, 512), np.float32)
        bk = t5_bucket_np(n)
        em[bk, npr] += 1.0
        em[31, :] -= 1.0
        em[:32, n < 0] = 0.0
        em[32, n < 0] = 1.0
        m["emat"] = em
        qpos = r * 128 + np.arange(128)[:, None]
        kpos = np.arange(256)[None, :]
        m["cmask"] = np.where(kpos <= qpos, 0.0, -1e30).astype(np.float32)
        maps.append(m)
    return maps


_CACHE = {}


def kernel(**inputs):
    maps = host_inputs(inputs)
    if "nc" not in _CACHE:
        _CACHE["nc"] = build()[0]
    res = run_bass_kernel_spmd(_CACHE["nc"], maps, core_ids=list(range(8)))
    out = np.empty((4, 32, 128, D), np.float32)
    for c in range(8):
        b, r = c // 2, c % 2
        out[b, r::2] = res.results[c]["out"]
    return out.reshape(4, S, D)
```

```python
from contextlib import ExitStack
import math
import numpy as np
import concourse.bass as bass
import concourse.mybir as mybir
from concourse.bass_utils import run_bass_kernel_spmd

F32 = mybir.dt.float32
BF16 = mybir.dt.bfloat16
AF = mybir.ActivationFunctionType
ALU = mybir.AluOpType
AX = mybir.AxisListType

D = 4096
S = 4096
NH = 16
NIH = 32
DFF = 11008
GW = 2048
DPLE = 256
EPS = 1e-6
NEG = -30000.0
IDX_SCALE = (128 ** -0.5) * (32 ** -0.5)
ATT_SCALE = 128 ** -0.5
NBLK = 16
TT = 1024
NB = 8


def tile_spec(K, widths, ks_list):
    spec = []
    off = 0
    for w in widths:
        kts = []
        kc0 = 0
        for ks in ks_list:
            kts.append((off, kc0, ks))
            off += 128 * ks * w
            kc0 += ks
        assert kc0 * 128 == K
        spec.append((w, kts))
    return spec, off


def pack_w(W, widths, ks_list):
    K, N = W.shape
    spec, total = tile_spec(K, widths, ks_list)
    out = np.empty(total, np.float32)
    Wr = W.reshape(K // 128, 128, N)
    n0 = 0
    for (w, kts) in spec:
        for (off, kc0, ks) in kts:
            out[off:off + 128 * ks * w] = Wr[kc0:kc0 + ks, :, n0:n0 + w].transpose(1, 0, 2).reshape(-1)
        n0 += w
    return out


KS4096 = [8, 8, 8, 8]
KS2048 = [8, 8]
KS5504 = [8, 8, 8, 8, 8, 3]
KS256 = [2]
W_INP = [256] * 8 + [256] * 16 + [32] + [256] * 8 + [256] * 8 + [256] * 16 + [256] * 16
NT_Q, NT_QI, NT_WI, NT_V, NT_U, NT_GA, NT_GB = 0, 8, 24, 25, 33, 41, 57
W_KVI = [128, 128, 128]
W_4096 = [256] * 16
W_GU = [256] * 86


class Tk:
    __slots__ = ("w", "r", "name")

    def __init__(self, name=""):
        self.w = None
        self.r = {}
        self.name = name


class Sched:
    def __init__(self, nc, es):
        self.nc = nc
        self.es = es
        self.eng = {"pe": nc.tensor, "act": nc.scalar, "dve": nc.vector, "pool": nc.gpsimd, "sp": nc.sync}
        self.sem = {k: es.enter_context(nc.semaphore("s_" + k)) for k in ["pe", "act", "dve", "pool"]}
        self.cnt = {k: 0 for k in self.sem}
        self.waited = {k: {} for k in self.eng}
        self.dsems = []

    def new_dsem(self, name):
        sem = self.es.enter_context(self.nc.semaphore("d_" + name))
        d = {"sem": sem, "tot": 0, "key": "d_" + name}
        self.dsems.append(d)
        return d

    def _wait(self, e, key, sem, val):
        if val <= 0 or self.waited[e].get(key, 0) >= val:
            return
        self.eng[e].wait_ge(sem, val)
        self.waited[e][key] = val

    def _deps(self, e, reads, writes):
        deps = {}

        def add(tok, kind):
            key, sem, val, owner = tok
            if owner == e and (e == "pe" or kind == "war"):
                return
            if deps.get(key, (None, 0))[1] < val:
                deps[key] = (sem, val)

        for t in reads:
            if t.w:
                add(t.w, "raw")
        for t in writes:
            if t.w:
                add(t.w, "waw")
            for tok in t.r.values():
                add(tok, "war")
        for key, (sem, val) in deps.items():
            self._wait(e, key, sem, val)

    def op(self, e, fn, reads=(), writes=(), sig=True):
        self._deps(e, reads, writes)
        ins = fn(self.eng[e])
        if sig:
            self.cnt[e] += 1
            ins.then_inc(self.sem[e], 1)
            tok = (e, self.sem[e], self.cnt[e], e)
        else:
            tok = (e, self.sem[e], self.cnt[e] + 1, e)
        for t in reads:
            old = t.r.get(e)
            if old is None or old[2] < tok[2]:
                t.r[e] = tok
        for t in writes:
            t.w = tok
            t.r = {}
        return ins

    def dma(self, q, d, out, in_, reads=(), writes=(), **kw):
        self._deps(q, reads, writes)
        self._wait(q, d["key"], d["sem"], d["tot"])
        ins = self.eng[q].dma_start(out=out, in_=in_, **kw)
        d["tot"] += 16
        ins.then_inc(d["sem"], 16)
        tok = (d["key"], d["sem"], d["tot"], "dma")
        for t in reads:
            t.r[d["key"]] = tok
        for t in writes:
            t.w = tok
            t.r = {}

    def barrier(self):
        for e in self.eng:
            for k in self.sem:
                self._wait(e, k, self.sem[k], self.cnt[k])
            for d in self.dsems:
                self._wait(e, d["key"], d["sem"], d["tot"])


def build(debug=False, stop_after=99):
    nc = bass.Bass("TRN2", target_bir_lowering=False)
    es = ExitStack()
    sc = Sched(nc, es)
    op, dma = sc.op, sc.dma

    def din(name, shape, dt=F32):
        return nc.dram_tensor(name, list(shape), dt, kind="ExternalInput").ap()

    def dscr(name, shape, dt):
        return nc.dram_tensor(name, list(shape), dt, kind=("ExternalOutput" if debug else "Internal")).ap()

    sp_inp, n_inp = tile_spec(D, W_INP, KS4096)
    sp_kvi, n_kvi = tile_spec(D, W_KVI, KS4096)
    sp_wa, n_wa = tile_spec(2048, W_4096, KS2048)
    sp_wb, n_wb = tile_spec(2048, W_4096, KS2048)
    sp_wo, n_wo = tile_spec(D, W_4096, KS4096)
    sp_gu, n_gu = tile_spec(D, W_GU, KS4096)
    sp_dn, n_dn = tile_spec(5504, W_4096, KS5504)
    sp_pg, n_pg = tile_spec(D, W_4096, KS4096)
    sp_pl, n_pl = tile_spec(DPLE, W_4096, KS256)

    x_seq = din("x_seq", [32, 128, D])
    x_own = din("x_own", [NBLK, 128, D])
    p_own = din("p_own", [NBLK, 128, DPLE])
    w_inp = din("w_inp", [n_inp])
    w_kvi = din("w_kvi", [n_kvi])
    w_a = din("w_a", [n_wa])
    w_b = din("w_b", [n_wb])
    w_o = din("w_o", [n_wo])
    w_gu = din("w_gu", [n_gu])
    w_dn = din("w_dn", [2, n_dn])
    w_pg = din("w_pg", [n_pg])
    w_pl = din("w_pl", [n_pl])
    gcols_d = din("gcols", [128, 3, 32])
    gqk_d = din("gqk", [128, 2, 128])
    lnrep_d = din("lnrep", [128, 2, GW])
    pgrep_d = din("pgrep", [128, D])
    wsT_d = din("wsT", [128, 8, 128])
    trilT_d = din("trilT", [128, 128])
    bs_d = din("bs", [128, 8])
    relb_d = din("relb", [32, NH])
    emat_d = din("emat", [33, 512])
    cmask_d = din("cmask", [128, 256])
    ident_d = din("ident", [128, 512])
    anti_d = din("anti", [128, 128])
    out_d = nc.dram_tensor("out", [NBLK, 128, D], F32, kind="ExternalOutput").ap()

    qT_d = dscr("qT", [2, NB, 128, NH, 128], BF16)
    qiT_d = dscr("qiT", [2, NB, 128, NIH, 128], BF16)
    ybT_d = dscr("ybT", [2, 16, 128, TT], BF16)
    sga_d = dscr("sga", [2, NB, 128, D], BF16)
    sgb_d = dscr("sgb", [2, NB, 128, D], BF16)
    x1_d = dscr("x1", [NBLK, 128, D], F32)
    x2_d = dscr("x2", [NBLK, 128, D], F32)
    actT_d = dscr("actT", [2, 86, 128, TT], BF16)
    tb_d = dscr("tbias", [NH, 512], F32)
    yaT_d = dscr("yaT", [2, 128, NH, TT], BF16) if debug else None
    kt_dbg = dscr("kt_dbg", [3, 128, S], BF16) if debug else None

    t_qT = [Tk() for _ in range(2)]
    t_qiT = [Tk() for _ in range(2)]
    t_ybT = [Tk() for _ in range(2)]
    t_sga = [Tk() for _ in range(2)]
    t_sgb = [Tk() for _ in range(2)]
    t_x1 = [[Tk() for _ in range(16)] for _ in range(2)]
    t_x2 = [[Tk() for _ in range(16)] for _ in range(2)]
    t_actT = [Tk() for _ in range(2)]
    t_tb = Tk()

    uid = [0]

    def sb(st, name, shape, dt):
        uid[0] += 1
        return st.enter_context(nc.sbuf_tensor("%s_%d" % (name, uid[0]), list(shape), dt))

    PGp = es.enter_context(nc.psum_tensor("PG", [128, 2048], F32))
    PTp = es.enter_context(nc.psum_tensor("PT", [128, 1024], F32))
    PMp = es.enter_context(nc.psum_tensor("PM", [128, 1024], F32))
    PG = PGp[:]
    PT = PTp[:]
    PM = PMp[:]
    PTb = PT.bitcast(BF16)
    t_PG = [Tk("pg%d" % i) for i in range(4)]
    t_PT = [Tk("pt%d" % i) for i in range(2)]
    t_PM = [Tk("pm%d" % i) for i in range(2)]

    IDF = sb(es, "IDF", [128, 128], F32)
    JREP = sb(es, "JREP", [128, 512], BF16)
    ONES = sb(es, "ONES", [128, 128], BF16)
    GCOLS = sb(es, "GCOLS", [128, 3, 32], F32)
    GQK = sb(es, "GQK", [128, 2, 128], F32)
    BS = sb(es, "BS", [128, 8], F32)
    WSM = sb(es, "WSM", [128, 8, 128], BF16)
    WB = sb(es, "WB", [128, 4, 8 * 256], BF16)
    KT = sb(es, "KT", [128, S], BF16)
    KIT = sb(es, "KIT", [128, S], BF16)
    VV = sb(es, "VV", [128, 32, 128], BF16)
    WABS = sb(es, "WABS", [128, NBLK, 32], F32)
    WSGN = sb(es, "WSGN", [128, NBLK, 32], F32)
    SM = sb(es, "SM", [128, 64], F32)
    t_const = Tk("const")
    t_WB = [Tk("wb%d" % i) for i in range(4)]
    d_WB = [sc.new_dsem("wb%d" % i) for i in range(4)]
    t_KV = Tk("kv")
    t_W = Tk("wabs")
    t_SM = Tk("sm")
    d_misc = sc.new_dsem("misc")
    d_ld = [sc.new_dsem("ld%d" % i) for i in range(4)]
    d_st = [sc.new_dsem("st%d" % i) for i in range(4)]
    IDB = JREP[:, 0:128]

    with ExitStack() as st:
        tmpf = sb(st, "c_tmpf", [128, 8, 128], F32)
        tmpi = sb(st, "c_tmpi", [128, 512], F32)
        tril = sb(st, "c_tril", [128, 128], F32)
        t_tmp = Tk()
        dma("sp", d_misc, IDF[:], ident_d[:, 0:128], writes=[t_const])
        dma("sp", d_misc, tmpi[:], ident_d[:, :], writes=[t_tmp])
        op("dve", lambda e: e.tensor_copy(out=JREP[:], in_=tmpi[:]), reads=[t_tmp], writes=[t_const])
        op("dve", lambda e: e.memset(ONES[:], 1.0), writes=[t_const])
        dma("sp", d_misc, GCOLS[:], gcols_d[:, :, :], writes=[t_const])
        dma("sp", d_misc, GQK[:], gqk_d[:, :, :], writes=[t_const])
        dma("sp", d_misc, BS[:], bs_d[:, :], writes=[t_const])
        dma("sp", d_misc, tmpf[:], wsT_d[:, :, :], writes=[t_tmp])
        dma("sp", d_misc, tril[:], trilT_d[:, :], writes=[t_tmp])
        op("dve", lambda e: e.tensor_tensor(out=WSM[:], in0=tmpf[:], in1=tril[:].unsqueeze(1).broadcast_to([128, 8, 128]),
                                            op=ALU.mult), reads=[t_tmp], writes=[t_const])
        relb = sb(st, "c_relb", [33, NH], F32)
        emat = sb(st, "c_emat", [33, 512], F32)
        tbs = sb(st, "c_tbs", [NH, 512], F32)
        op("dve", lambda e: e.memset(relb[:], NEG), writes=[t_tmp])
        dma("sp", d_misc, relb[0:32, :], relb_d[:, :], reads=[], writes=[t_tmp])
        dma("sp", d_misc, emat[:], emat_d[:, :], writes=[t_tmp])
        op("pe", lambda e: e.matmul(PG[0:NH, 0:512], lhsT=relb[:], rhs=emat[:], start=True, stop=True),
           reads=[t_tmp], writes=[t_PG[0]])
        op("act", lambda e: e.activation(out=tbs[:], in_=PG[0:NH, 0:512], func=AF.Copy), reads=[t_PG[0]], writes=[t_tmp])
        dma("sp", d_misc, tb_d[:, :], tbs[:], reads=[t_tmp], writes=[t_tb])
        sc.barrier()

    wslot = [0]

    def gemm_ntile(inT, t_in, KC, wflat, spec, nt, nblk=NB):
        width, kts = spec[nt]
        for (off, kc0, KS) in kts:
            s = wslot[0] % 4
            wslot[0] += 1
            src = wflat[off:off + 128 * KS * width].rearrange("(p k) -> p k", p=128)
            dma("pool", d_WB[s], WB[:, s, 0:KS * width], src, writes=[t_WB[s]], max_dma_last_dim=8192)
            if kc0 + KS == KC:
                order = [(ks, blk) for blk in range(nblk) for ks in range(KS)]
            else:
                order = [(ks, blk) for ks in range(KS) for blk in range(nblk)]
            for oi, (ks, blk) in enumerate(order):
                kc = kc0 + ks
                b, sub = blk // 2, blk % 2
                last_of_tile = (oi == len(order) - 1)
                last_k = (kc == KC - 1)
                o = PG[:, b * 512 + sub * 256: b * 512 + sub * 256 + width]
                l = inT[:, kc, blk * 128:(blk + 1) * 128]
                r = WB[:, s, ks * width:(ks + 1) * width]
                op("pe", lambda e, o=o, l=l, r=r, st_=(kc == 0 and sub == 0), sp_=last_k:
                   e.matmul(o, lhsT=l, rhs=r, start=st_, stop=sp_, skip_group_check=True),
                   reads=[t_in, t_WB[s]], writes=[t_PG[b]], sig=(last_of_tile or last_k))
        return width

    def bankv(b, width):
        return PG[:, b * 512:(b + 1) * 512].rearrange("p (s c) -> p s c", s=2)[:, :, 0:width]

    def transposes_bf16(srcs, t_src, ptbank):
        for j, s_ap in enumerate(srcs):
            o = PTb[:, ptbank * 1024 + j * 128: ptbank * 1024 + (j + 1) * 128]
            op("pe", lambda e, o=o, s_ap=s_ap: e.transpose(o, s_ap, IDB), reads=[t_src, t_const],
               writes=[t_PT[ptbank]], sig=(j == len(srcs) - 1))

    def rstd_from_ss(ssap, n, dim):
        op("act", lambda e: e.activation(out=ssap, in_=ssap, func=AF.Sqrt, scale=1.0 / dim, bias=EPSC[:, 0:1]),
           reads=[t_SM, t_const], writes=[t_SM])
        op("dve", lambda e: e.reciprocal(out=ssap, in_=ssap), reads=[t_SM], writes=[t_SM])

    def rstd_act(ssap, dim):
        op("act", lambda e: e.activation(out=ssap, in_=ssap, func=AF.Ln, scale=1.0 / dim, bias=EPSC[:, 0:1]),
           reads=[t_SM, t_const], writes=[t_SM])
        op("act", lambda e: e.activation(out=ssap, in_=ssap, func=AF.Exp, scale=-0.5), reads=[t_SM], writes=[t_SM])

    EPSC = sb(es, "EPSC", [128, 1], F32)
    op("dve", lambda e: e.memset(EPSC[:], EPS), writes=[t_const])

    def norm_transpose(st_parent, srcs, gi, hT, t_hT):
        with ExitStack() as st:
            XB = sb(st, "nt_xb", [128, 2, D], F32)
            JK = sb(st, "nt_jk", [128, D], BF16)
            t_XB = [Tk(), Tk()]
            t_JK = Tk()
            def nt_front(bi):
                src, tsrc = srcs[bi]
                s = bi % 2
                dma("sp", d_ld[s], XB[:, s, :], src, reads=[tsrc], writes=[t_XB[s]])
                ssc = SM[:, s:s + 1]
                op("act", lambda e: e.activation(out=JK[:], in_=XB[:, s, :], func=AF.Square, accum_out=ssc),
                   reads=[t_XB[s]], writes=[t_JK, t_SM])
                rstd_act(ssc, D)
                op("pool", lambda e: e.tensor_scalar(out=XB[:, s, :], in0=XB[:, s, :], scalar1=ssc, scalar2=1.0,
                                                     op0=ALU.mult, op1=ALU.mult), reads=[t_XB[s], t_SM], writes=[t_XB[s]])

            def nt_back(bi):
                s = bi % 2
                for g4 in range(8):
                    pm = g4 % 2
                    for j in range(4):
                        kc = g4 * 4 + j
                        o = PM[:, pm * 512 + j * 128: pm * 512 + (j + 1) * 128]
                        i_ = XB[:, s, kc * 128:(kc + 1) * 128]
                        op("pe", lambda e, o=o, i_=i_: e.transpose(o, i_, IDF[:]), reads=[t_XB[s], t_const],
                           writes=[t_PM[pm]], sig=(j == 3))
                    o = hT[:, g4 * 4:(g4 + 1) * 4, bi * 128:(bi + 1) * 128]
                    i_ = PM[:, pm * 512:(pm + 1) * 512].rearrange("p (j c) -> p j c", j=4)
                    gb = GCOLS[:, gi, g4 * 4:(g4 + 1) * 4].unsqueeze(2).broadcast_to([128, 4, 128])
                    op("dve", lambda e, o=o, i_=i_, gb=gb: e.tensor_tensor(out=o, in0=i_, in1=gb, op=ALU.mult),
                       reads=[t_PM[pm], t_const], writes=[t_hT])

            nt_front(0)
            for bi in range(len(srcs)):
                if bi + 1 < len(srcs):
                    nt_front(bi + 1)
                nt_back(bi)
            sc.barrier()

    t_xseq = Tk()
    with ExitStack() as st:
        WKV = sb(st, "s0_wkv", [128, 32, 384], BF16)
        XB = sb(st, "s0_xb", [128, 3, D], F32)
        JK = sb(st, "s0_jk", [128, D], BF16)
        HB = sb(st, "s0_hb", [128, 2, 32, 128], BF16)
        KN = sb(st, "s0_kn", [128, 2, 2, 128], BF16)
        SQ = sb(st, "s0_sq", [128, 128], F32)
        t_WKV, t_JK, t_SQ = Tk(), Tk(), Tk()
        t_XB = [Tk(), Tk(), Tk()]
        t_HB = [Tk(), Tk()]
        t_KN = [Tk(), Tk()]
        for nt in range(3):
            for (off, kc0, KS) in sp_kvi[nt][1]:
                srcw = w_kvi[off:off + 128 * KS * 128].rearrange("(p k c) -> p k c", p=128, k=KS)
                dma("pool", d_WB[0], WKV[:, kc0:kc0 + KS, nt * 128:(nt + 1) * 128], srcw, writes=[t_WKV], max_dma_last_dim=8192)
        def s0_front(blk):
            s = blk % 3
            dma("sp", d_ld[s], XB[:, s, :], x_seq[blk], reads=[t_xseq], writes=[t_XB[s]])
            ssc = SM[:, s:s + 1]
            op("act", lambda e: e.activation(out=JK[:], in_=XB[:, s, :], func=AF.Square, accum_out=ssc),
               reads=[t_XB[s]], writes=[t_JK, t_SM])
            rstd_act(ssc, D)
            op("pool", lambda e: e.tensor_scalar(out=XB[:, s, :], in0=XB[:, s, :], scalar1=ssc, scalar2=1.0, op0=ALU.mult, op1=ALU.mult),
               reads=[t_XB[s], t_SM], writes=[t_XB[s]])

        def s0_back(blk):
            s = blk % 2
            x3 = blk % 3
            for g4 in range(8):
                pm = g4 % 2
                for j in range(4):
                    kc = g4 * 4 + j
                    o = PM[:, pm * 512 + j * 128: pm * 512 + (j + 1) * 128]
                    i_ = XB[:, x3, kc * 128:(kc + 1) * 128]
                    op("pe", lambda e: e.transpose(o, i_, IDF[:]), reads=[t_XB[x3], t_const], writes=[t_PM[pm]], sig=(j == 3))
                o = HB[:, s, g4 * 4:(g4 + 1) * 4, :]
                i_ = PM[:, pm * 512:(pm + 1) * 512].rearrange("p (j c) -> p j c", j=4)
                gb = GCOLS[:, 0, g4 * 4:(g4 + 1) * 4].unsqueeze(2).broadcast_to([128, 4, 128])
                op("dve", lambda e: e.tensor_tensor(out=o, in0=i_, in1=gb, op=ALU.mult), reads=[t_PM[pm], t_const], writes=[t_HB[s]])
            for kc in range(32):
                op("pe", lambda e: e.matmul(PG[:, s * 512: s * 512 + 384], lhsT=HB[:, s, kc, :], rhs=WKV[:, kc, :], start=(kc == 0), stop=(kc == 31)),
                   reads=[t_HB[s], t_WKV], writes=[t_PG[s]], sig=(kc == 31))

        def s0_epi(blk):
            s = blk % 2
            pgk = PG[:, s * 512: s * 512 + 128]
            pgv = PG[:, s * 512 + 128: s * 512 + 256]
            pgi = PG[:, s * 512 + 256: s * 512 + 384]
            ss2 = SM[:, 8 + s: 9 + s]
            op("act", lambda e: e.activation(out=SQ[:], in_=pgk, func=AF.Square, accum_out=ss2), reads=[t_PG[s]], writes=[t_SQ, t_SM])
            rstd_from_ss(ss2, 1, 128)
            op("dve", lambda e: e.scalar_tensor_tensor(out=KN[:, s, 0, :], in0=pgk, scalar=ss2, in1=GQK[:, 1, :], op0=ALU.mult, op1=ALU.mult),
               reads=[t_PG[s], t_SM, t_const], writes=[t_KN[s]])
            op("act", lambda e: e.activation(out=KN[:, s, 1, :], in_=pgi, func=AF.Copy), reads=[t_PG[s]], writes=[t_KN[s]])
            op("act", lambda e: e.activation(out=VV[:, blk, :], in_=pgv, func=AF.Copy), reads=[t_PG[s]], writes=[t_KV])
            for j in range(2):
                o = PTb[:, s * 1024 + j * 128: s * 1024 + (j + 1) * 128]
                op("pe", lambda e: e.transpose(o, KN[:, s, j, :], IDB), reads=[t_KN[s], t_const], writes=[t_PT[s]], sig=(j == 1))
            op("act", lambda e: e.activation(out=KT[:, blk * 128:(blk + 1) * 128], in_=PTb[:, s * 1024: s * 1024 + 128], func=AF.Copy),
               reads=[t_PT[s]], writes=[t_KV])
            op("act", lambda e: e.activation(out=KIT[:, blk * 128:(blk + 1) * 128], in_=PTb[:, s * 1024 + 128: s * 1024 + 256], func=AF.Copy),
               reads=[t_PT[s]], writes=[t_KV])

        s0_front(0)
        s0_front(1)
        s0_back(0)
        for blk in range(32):
            if blk + 2 < 32:
                s0_front(blk + 2)
            if blk + 1 < 32:
                s0_back(blk + 1)
            s0_epi(blk)
        sc.barrier()
        if debug:
            dma("sp", d_misc, kt_dbg[0], KT[:], reads=[t_KV])
            dma("sp", d_misc, kt_dbg[1], KIT[:], reads=[t_KV])
            dma("sp", d_misc, kt_dbg[2], VV[:].rearrange("p a b -> p (a b)"), reads=[t_KV])
        sc.barrier()

    t_xown = Tk()
    t_pown = Tk()
    t_out = Tk()
    sc_d = dscr("scores", [2, NB, 128, S], F32)
    ns_d = dscr("negsel", [2, NB, 128, S], BF16)
    t_scd = [[Tk() for _ in range(NB)] for _ in range(2)]
    t_nsd = [[Tk() for _ in range(NB)] for _ in range(2)]
    d_tk = sc.new_dsem("tk")
    d_sc = [sc.new_dsem("sc0"), sc.new_dsem("sc1")]
    state = {}
    t_dummy = Tk()
    DB = [0, 1, 3]

    def run(g):
        for _ in g:
            pass

    def s1(tt, pump=None):
        with ExitStack() as st:
            HT = sb(st, "s1_ht", [128, 32, TT], BF16)
            t_HT = Tk()
            norm_transpose(st, [(x_own[tt * 8 + bi], t_xown) for bi in range(8)], 0, HT, t_HT)
            VB = sb(st, "s1_vb", [128, NB, GW], BF16)
            t_VB = [Tk() for _ in range(NB)]
            STG = sb(st, "s1_stg", [128, 2, NB, 256], BF16)
            t_STG = [Tk(), Tk()]
            TS = sb(st, "s1_ts", [128, 2, 2, TT], BF16)
            t_TS = [Tk(), Tk()]
            SQ = sb(st, "s1_sq", [128, 4, 128], F32)
            t_SQ = Tk()
            LNR = sb(st, "s1_lnr", [128, 2, GW], F32)
            t_LNR = Tk()
            dma("sp", d_misc, LNR[:], lnrep_d[:, :, :], writes=[t_LNR])
            BNS = sb(st, "s1_bns", [128, 4, 6], F32)
            MV = sb(st, "s1_mv", [128, 2], F32)
            t_BN = Tk()
            GU = sb(st, "s1_gu", [128, 2, 2, 256], BF16)
            t_GU = [Tk(), Tk()]
            cnt = [0]

            def tm_to_dram_T(nt_local, dst5, t_dst, slot):
                for hh in range(2):
                    transposes_bf16([STG[:, slot, blk, hh * 128:(hh + 1) * 128] for blk in range(8)], t_STG[slot], hh)
                    op("act", lambda e, hh=hh: e.activation(out=TS[:, slot, hh, :], in_=PTb[:, hh * 1024:(hh + 1) * 1024],
                                                           func=AF.Copy), reads=[t_PT[hh]], writes=[t_TS[slot]])
                h0 = 2 * nt_local
                for hh in range(2):
                    dma("sp", d_st[slot], dst5[tt, :, :, h0 + hh, :].rearrange("b d q -> d b q"),
                        TS[:, slot, hh, :].rearrange("d (b q) -> d b q", b=NB), reads=[t_TS[slot]], writes=[t_dst])

            def sgu_mix():
                for blk in range(8):
                    for g in range(8):
                        b, sub = g // 2, g % 2
                        o = PG[:, b * 512 + sub * 256: b * 512 + (sub + 1) * 256]
                        op("pe", lambda e, o=o, g=g, blk=blk: e.matmul(o, lhsT=WSM[:, g, :], rhs=VB[:, blk, g * 256:(g + 1) * 256],
                                                                    start=True, stop=True, skip_group_check=True),
                           reads=[t_const, t_VB[blk]], writes=[t_PG[b]])
                    for g in range(8):
                        b, sub = g // 2, g % 2
                        i_ = PG[:, b * 512 + sub * 256: b * 512 + (sub + 1) * 256]
                        op("act", lambda e, i_=i_, g=g, blk=blk: e.activation(out=VB[:, blk, g * 256:(g + 1) * 256], in_=i_,
                                                                           func=AF.Identity, bias=BS[:, g:g + 1], scale=1.0),
                           reads=[t_PG[b], t_const], writes=[t_VB[blk]])

            S1_ORDER = list(range(0, NT_U)) + list(range(NT_GA, len(W_INP))) + list(range(NT_U, NT_GA))
            for nt in S1_ORDER:
                if pump is not None and nt > 0:
                    pump()
                width = gemm_ntile(HT, t_HT, 32, w_inp, sp_inp, nt)
                slot = cnt[0] % 2
                cnt[0] += 1
                if nt < NT_QI:
                    for b in range(4):
                        op("act", lambda e, b=b: e.activation(out=SQ[:].rearrange("p (s h) c -> p s (h c)", s=2),
                                                              in_=bankv(b, 256), func=AF.Square),
                           reads=[t_PG[b]], writes=[t_SQ])
                        op("dve", lambda e, b=b: e.tensor_reduce(out=SM[:, 16 + 4 * b: 20 + 4 * b], in_=SQ[:],
                                                                 axis=AX.X, op=ALU.add), reads=[t_SQ], writes=[t_SM])
                    rstd_from_ss(SM[:, 16:32], 16, 128)
                    for blk in range(8):
                        b, sub = blk // 2, blk % 2
                        for hh in range(2):
                            i_ = PG[:, b * 512 + sub * 256 + hh * 128: b * 512 + sub * 256 + (hh + 1) * 128]
                            c = 16 + 4 * b + 2 * sub + hh
                            op("dve", lambda e, blk=blk, hh=hh, i_=i_, c=c: e.scalar_tensor_tensor(
                                out=STG[:, slot, blk, hh * 128:(hh + 1) * 128], in0=i_, scalar=SM[:, c:c + 1],
                                in1=GQK[:, 0, :], op0=ALU.mult, op1=ALU.mult),
                               reads=[t_PG[b], t_SM, t_const], writes=[t_STG[slot]])
                    tm_to_dram_T(nt - NT_Q, qT_d, t_qT[tt], slot)
                elif nt < NT_WI:
                    for b in range(4):
                        op("act", lambda e, b=b: e.activation(out=STG[:, slot, 2 * b:2 * b + 2, :], in_=bankv(b, 256),
                                                              func=AF.Copy), reads=[t_PG[b]], writes=[t_STG[slot]])
                    tm_to_dram_T(nt - NT_QI, qiT_d, t_qiT[tt], slot)
                elif nt == NT_WI:
                    for b in range(4):
                        i0 = tt * 8 + 2 * b
                        op("act", lambda e, b=b, i0=i0: e.activation(out=WABS[:, i0:i0 + 2, :], in_=bankv(b, 32), func=AF.Abs,
                                                                     scale=IDX_SCALE), reads=[t_PG[b]], writes=[t_W])
                        op("act", lambda e, b=b, i0=i0: e.activation(out=WSGN[:, i0:i0 + 2, :], in_=bankv(b, 32), func=AF.Sign),
                           reads=[t_PG[b]], writes=[t_W])
                elif nt < NT_U:
                    c0 = (nt - NT_V) * 256
                    for b in range(4):
                        op("act", lambda e, b=b, c0=c0: e.activation(out=VB[:, 2 * b:2 * b + 2, c0:c0 + 256], in_=bankv(b, 256),
                                                                     func=AF.Gelu_apprx_tanh),
                           reads=[t_PG[b]], writes=[t_VB[2 * b], t_VB[2 * b + 1]])
                    if nt == NT_U - 1:
                        for blk in range(8):
                            for c4 in range(4):
                                op("dve", lambda e, c4=c4, blk=blk: e.bn_stats(out=BNS[:, c4, :], in_=VB[:, blk, c4 * 512:(c4 + 1) * 512]),
                                   reads=[t_VB[blk]], writes=[t_BN])
                            op("dve", lambda e: e.bn_aggr(out=MV[:], in_=BNS[:]), reads=[t_BN], writes=[t_BN])
                            op("act", lambda e: e.activation(out=SM[:, 32:33], in_=MV[:, 1:2], func=AF.Sqrt, scale=1.0, bias=EPSC[:, 0:1]),
                               reads=[t_BN, t_const], writes=[t_SM])
                            op("dve", lambda e: e.reciprocal(out=SM[:, 32:33], in_=SM[:, 32:33]), reads=[t_SM], writes=[t_SM])
                            op("dve", lambda e: e.scalar_tensor_tensor(out=SM[:, 33:34], in0=MV[:, 0:1], scalar=-1.0, in1=SM[:, 32:33],
                                                                       op0=ALU.mult, op1=ALU.mult), reads=[t_SM, t_BN], writes=[t_SM])
                            op("act", lambda e, blk=blk: e.activation(out=VB[:, blk, :], in_=VB[:, blk, :], func=AF.Identity,
                                                                      scale=SM[:, 32:33], bias=SM[:, 33:34]),
                               reads=[t_VB[blk], t_SM], writes=[t_VB[blk]])
                            op("dve", lambda e, blk=blk: e.tensor_tensor(out=VB[:, blk, :], in0=VB[:, blk, :], in1=LNR[:, 0, :], op=ALU.mult),
                               reads=[t_VB[blk], t_LNR], writes=[t_VB[blk]])
                            op("dve", lambda e, blk=blk: e.tensor_tensor(out=VB[:, blk, :], in0=VB[:, blk, :], in1=LNR[:, 1, :], op=ALU.add),
                               reads=[t_VB[blk], t_LNR], writes=[t_VB[blk]])
                elif nt < NT_GA:
                    c0 = (nt - NT_U) * 256
                    for b in range(4):
                        op("act", lambda e, b=b: e.activation(out=GU[:, b % 2, :, :], in_=bankv(b, 256), func=AF.Gelu_apprx_tanh),
                           reads=[t_PG[b]], writes=[t_GU[b % 2]])
                        op("dve", lambda e, b=b, c0=c0: e.tensor_tensor(out=STG[:, slot, 2 * b:2 * b + 2, :], in0=GU[:, b % 2, :, :],
                                                                        in1=VB[:, 2 * b:2 * b + 2, c0:c0 + 256], op=ALU.mult),
                           reads=[t_GU[b % 2], t_VB[2 * b], t_VB[2 * b + 1]], writes=[t_STG[slot]])
                    for hh in range(2):
                        transposes_bf16([STG[:, slot, blk, hh * 128:(hh + 1) * 128] for blk in range(8)], t_STG[slot], hh)
                        op("act", lambda e, hh=hh: e.activation(out=TS[:, slot, hh, :], in_=PTb[:, hh * 1024:(hh + 1) * 1024],
                                                               func=AF.Copy), reads=[t_PT[hh]], writes=[t_TS[slot]])
                    k0 = 2 * (nt - NT_U)
                    dma("sp", d_st[slot], ybT_d[tt, k0:k0 + 2, :, :].rearrange("k p t -> p k t"), TS[:, slot, :, :],
                        reads=[t_TS[slot]], writes=[t_ybT[tt]])
                else:
                    isb = nt >= NT_GB
                    c0 = (nt - (NT_GB if isb else NT_GA)) * 256
                    for b in range(4):
                        op("act", lambda e, b=b: e.activation(out=STG[:, slot, 2 * b:2 * b + 2, :], in_=bankv(b, 256),
                                                              func=AF.Sigmoid), reads=[t_PG[b]], writes=[t_STG[slot]])
                    dst = (sgb_d if isb else sga_d)[tt, :, :, c0:c0 + 256].rearrange("b p c -> p b c")
                    dma("sp", d_st[slot], dst, STG[:, slot, :, :], reads=[t_STG[slot]],
                        writes=[(t_sgb if isb else t_sga)[tt]])
                    if nt == NT_GB - 1:
                        sgu_mix()
            sc.barrier()

    def s3_idx(tt):
        with ExitStack() as st:
            QI = sb(st, "a_qi", [128, 1, NIH, 128], BF16)
            SCO = sb(st, "a_sc", [128, 2, S], F32)
            RR = sb(st, "a_rr", [128, 6, 512], BF16)
            CM = sb(st, "a_cm", [128, 256], F32)
            DBUF = [(PG[:, 0:512], t_PG[0]), (PG[:, 512:1024], t_PG[1]), (PG[:, 1536:2048], t_PG[3]),
                    (PT[:, 0:512], t_PT[0]), (PT[:, 512:1024], t_PT[1]), (PM[:, 0:512], t_PM[0])]
            DIAG = WB[:, 0:2, :].rearrange("p s (h c) -> p (s h) c", h=16)
            t_CM, t_DIAG = Tk(), Tk()
            t_QI = [Tk()]
            t_SC = [Tk(), Tk()]
            t_RR = [Tk() for _ in range(6)]
            dma("sp", d_misc, CM[:], cmask_d[:, :], writes=[t_CM])

            def idx_phase(il):
                i = tt * 8 + il
                par = il % 2
                N = (2 * i + 2) * 128
                dma("sp", d_ld[par], QI[:, 0], qiT_d[tt, il], reads=[t_qiT[tt]], writes=[t_QI[0]])
                op("pool", lambda e: e.tensor_tensor(out=DIAG, in0=IDB.unsqueeze(1).broadcast_to([128, NIH, 128]),
                                                     in1=WSGN[:, i, :].unsqueeze(2).broadcast_to([128, NIH, 128]), op=ALU.mult),
                   reads=[t_const, t_W], writes=[t_DIAG])
                for cc in range((N + 511) // 512):
                    c0 = cc * 512
                    w = min(512, N - c0)
                    LAG = 4
                    for h in range(NIH + LAG):
                        if h < NIH:
                            bk = h % 6
                            dap, dtk = DBUF[bk]
                            op("pe", lambda e, dap=dap, h=h: e.matmul(dap[:, 0:w], lhsT=QI[:, 0, h, :], rhs=KIT[:, c0:c0 + w],
                                                                    start=True, stop=True), reads=[t_QI[0], t_KV], writes=[dtk])
                            if h % 3 == 2:
                                op("dve", lambda e, bk=bk, dap=dap, h=h: e.tensor_scalar(
                                    out=RR[:, bk, 0:w], in0=dap[:, 0:w], scalar1=WABS[:, i, h:h + 1], scalar2=0.0,
                                    op0=ALU.mult, op1=ALU.max), reads=[dtk, t_W], writes=[t_RR[bk]])
                            else:
                                op("act", lambda e, bk=bk, dap=dap, h=h: e.activation(out=RR[:, bk, 0:w], in_=dap[:, 0:w], func=AF.Relu,
                                                                                   scale=WABS[:, i, h:h + 1]), reads=[dtk, t_W], writes=[t_RR[bk]])
                        if h >= LAG:
                            hp = h - LAG
                            op("pe", lambda e, hp=hp: e.matmul(PG[:, 1024:1024 + w], lhsT=DIAG[:, hp, :], rhs=RR[:, hp % 6, 0:w],
                                                               start=(hp == 0), stop=(hp == NIH - 1)),
                               reads=[t_DIAG, t_RR[hp % 6]], writes=[t_PG[2]])
                        yield
                    op("act", lambda e: e.activation(out=SCO[:, par, c0:c0 + w], in_=PG[:, 1024:1024 + w], func=AF.Copy),
                       reads=[t_PG[2]], writes=[t_SC[par]])


            for il in range(NB):
                N = (2 * (tt * 8 + il) + 2) * 128
                par = il % 2
                run(idx_phase(il))
                op("dve", lambda e: e.tensor_tensor(out=SCO[:, par, N - 256:N], in0=SCO[:, par, N - 256:N], in1=CM[:], op=ALU.add),
                   reads=[t_SC[par], t_CM], writes=[t_SC[par]])
                dma("sp", d_sc[par], sc_d[tt, il][:, 0:N], SCO[:, par, 0:N], reads=[t_SC[par]], writes=[t_scd[tt][il]])
            sc.barrier()

    def topk_gen(tt, SCO1, NS1, M8):
        t_S, t_N, t_M = Tk(), Tk(), Tk()
        for il in range(NB):
            N = (2 * (tt * 8 + il) + 2) * 128
            dma("sp", d_tk, SCO1[:, 0:N], sc_d[tt, il][:, 0:N], reads=[t_scd[tt][il]], writes=[t_S])
            yield
            for r8 in range(32):
                op("dve", lambda e, r8=r8: e.max(out=M8[:, r8 * 8:(r8 + 1) * 8], in_=SCO1[:, 0:N]), reads=[t_S], writes=[t_M])
                yield
                if r8 < 31:
                    op("dve", lambda e, r8=r8: e.match_replace(out=SCO1[:, 0:N], in_to_replace=M8[:, r8 * 8:(r8 + 1) * 8],
                                                               in_values=SCO1[:, 0:N], imm_value=-3.0e38),
                       reads=[t_S, t_M], writes=[t_S])
                    yield
            dma("sp", d_tk, SCO1[:, 0:N], sc_d[tt, il][:, 0:N], reads=[t_scd[tt][il]], writes=[t_S])
            op("dve", lambda e: e.tensor_scalar(out=NS1[:, 0:N], in0=SCO1[:, 0:N], scalar1=M8[:, 255:256], scalar2=NEG,
                                                op0=ALU.is_lt, op1=ALU.mult), reads=[t_S, t_M], writes=[t_N])
            yield
            dma("sp", d_tk, ns_d[tt, il][:, 0:N], NS1[:, 0:N], reads=[t_N], writes=[t_nsd[tt][il]])
            yield

    def s3_attn(tt):
        stt = ExitStack()
        YAT = sb(stt, "yat", [128, NH, TT], BF16)
        t_YAT = Tk()
        state[tt] = (stt, YAT, t_YAT)
        with ExitStack() as st:
            QQ = sb(st, "a_qq", [128, 3, NH, 128], BF16)
            NS = sb(st, "a_ns", [128, 2, S], BF16)
            BTH = sb(st, "a_bth", [128, 3, NH * 128], BF16)
            BTL = sb(st, "a_btl", [128, 3, NH * 128], BF16)
            PTs = sb(st, "a_pt", [128, 3, 512], BF16)
            PVs = sb(st, "a_pvs", [128, 2, 512], F32)
            LNs = sb(st, "a_lns", [128, 2, 512], F32)
            BTF = sb(st, "a_btf", [128, 512], F32)
            ANTI = sb(st, "a_anti", [128, 128], F32)
            t_PVs = [Tk(), Tk()]
            t_LNs = [Tk(), Tk()]
            t_BTF, t_BT, t_CM, t_TMPB = Tk(), Tk(), Tk(), Tk()
            t_QQ = [Tk(), Tk(), Tk()]
            t_NS = [Tk(), Tk()]
            t_PTs = [Tk(), Tk(), Tk()]
            dma("sp", d_misc, ANTI[:], anti_d[:, :], writes=[t_CM])

            with ExitStack() as st_b:
                TMPB = WB[:, 2:4, :].bitcast(F32).rearrange("p s (h c) -> p (s h) c", h=8)
                for m in range(3):
                    srcb = bass.AP(tensor=tb_d.tensor, offset=m * 128, ap=[[1, 128], [512, NH], [1, 128]])
                    dma("sp", d_misc, TMPB, srcb, reads=[t_tb], writes=[t_TMPB])
                    for j in range(4):
                        op("pe", lambda e, j=j: e.matmul(PG[:, j * 512:(j + 1) * 512], lhsT=ANTI[:], rhs=TMPB[:, 4 * j:4 * j + 4, :],
                                                          start=True, stop=True), reads=[t_CM, t_TMPB], writes=[t_PG[j]])
                        op("act", lambda e, j=j: e.activation(out=BTF[:], in_=PG[:, j * 512:(j + 1) * 512], func=AF.Copy, scale=1.0 / ATT_SCALE),
                           reads=[t_PG[j]], writes=[t_BTF])
                        op("dve", lambda e, j=j, m=m: e.tensor_copy(out=BTH[:, m, j * 512:(j + 1) * 512], in_=BTF[:]), reads=[t_BTF], writes=[t_BT])
                        op("dve", lambda e, j=j, m=m: e.tensor_tensor(out=BTL[:, m, j * 512:(j + 1) * 512], in0=BTF[:],
                                                                      in1=BTH[:, m, j * 512:(j + 1) * 512], op=ALU.subtract),
                           reads=[t_BTF, t_BT], writes=[t_BT])

            def attn_phase(il):
                i = tt * 8 + il
                par = il % 2
                NCH = 2 * i + 2
                units = [(hg, c) for hg in range(4) for c in range(NCH)]

                def emit_L(k):
                    hg, c = units[k]
                    lb = k % 3
                    o = PG[:, lb * 512:(lb + 1) * 512]
                    op("pe", lambda e: e.matmul(o, lhsT=KT[:, c * 128:(c + 1) * 128], rhs=QQ[:, il % 3, 4 * hg:4 * hg + 4, :],
                                                start=True, stop=False),
                       reads=[t_KV, t_QQ[il % 3]], writes=[t_PG[lb]], sig=False)
                    m = NCH - 1 - c
                    near = m < 3
                    op("pe", lambda e: e.matmul(o, lhsT=NS[:, par, c * 128:(c + 1) * 128], rhs=JREP[:], start=False, stop=(not near)),
                       reads=[t_NS[par], t_const], writes=[t_PG[lb]], sig=(not near))
                    if near:
                        op("pe", lambda e: e.matmul(o, lhsT=IDB, rhs=BTH[:, m, hg * 512:(hg + 1) * 512], start=False, stop=False),
                           reads=[t_BT, t_const], writes=[t_PG[lb]], sig=False)
                        op("pe", lambda e: e.matmul(o, lhsT=IDB, rhs=BTL[:, m, hg * 512:(hg + 1) * 512], start=False, stop=True),
                           reads=[t_BT, t_const], writes=[t_PG[lb]])

                def emit_EV(k):
                    hg, c = units[k]
                    lb = k % 3
                    o = PG[:, lb * 512:(lb + 1) * 512]
                    op("act", lambda e: e.activation(out=PTs[:, lb, :], in_=o, func=AF.Exp, scale=ATT_SCALE),
                       reads=[t_PG[lb]], writes=[t_PTs[lb]])
                    op("pe", lambda e: e.matmul(PM[:, 0:512], lhsT=VV[:, c, :], rhs=PTs[:, lb, :], start=(c == 0), stop=(c == NCH - 1)),
                       reads=[t_KV, t_PTs[lb]], writes=[t_PM[0]], sig=False)
                    op("pe", lambda e: e.matmul(PM[:, 512:1024], lhsT=ONES[:], rhs=PTs[:, lb, :], start=(c == 0), stop=(c == NCH - 1)),
                       reads=[t_const, t_PTs[lb]], writes=[t_PM[1]])
                    if c == NCH - 1:
                        p2 = hg % 2
                        op("act", lambda e: e.activation(out=LNs[:, p2, :], in_=PM[:, 512:1024], func=AF.Ln), reads=[t_PM[1]], writes=[t_LNs[p2]])
                        op("act", lambda e: e.activation(out=LNs[:, p2, :], in_=LNs[:, p2, :], func=AF.Exp, scale=-1.0),
                           reads=[t_LNs[p2]], writes=[t_LNs[p2]])
                        op("act", lambda e: e.activation(out=PVs[:, p2, :], in_=PM[:, 0:512], func=AF.Copy), reads=[t_PM[0]], writes=[t_PVs[p2]])
                        op("pool", lambda e: e.tensor_tensor(
                            out=YAT[:, hg * 4:(hg + 1) * 4, il * 128:(il + 1) * 128], in0=PVs[:, p2, :].rearrange("p (h q) -> p h q", h=4),
                            in1=LNs[:, p2, :].rearrange("p (h q) -> p h q", h=4), op=ALU.mult),
                           reads=[t_PVs[p2], t_LNs[p2]], writes=[t_YAT])

                emit_L(0)
                emit_L(1)
                for k in range(len(units)):
                    if k + 2 < len(units):
                        emit_L(k + 2)
                    emit_EV(k)
                    yield

            def preload(il):
                N = (2 * (tt * 8 + il) + 2) * 128
                dma("sp", d_ld[2 + il % 2], QQ[:, il % 3], qT_d[tt, il], reads=[t_qT[tt]], writes=[t_QQ[il % 3]])
                dma("sp", d_ld[il % 2], NS[:, il % 2, 0:N], ns_d[tt, il][:, 0:N], reads=[t_nsd[tt][il]], writes=[t_NS[il % 2]])

            preload(0)
            for il in range(NB):
                if il + 1 < NB:
                    preload(il + 1)
                run(attn_phase(il))
            if debug:
                dma("sp", d_misc, yaT_d[tt], YAT[:], reads=[t_YAT])
            sc.barrier()

    def s45(tt):
        stt, YAT, t_YAT = state[tt]
        with ExitStack() as st:
            MT = sb(st, "b_mt", [128, 32, TT], BF16)
            st4 = ExitStack()
            YBT = sb(st4, "b_ybt", [128, 16, TT], BF16)
            TMPM = sb(st4, "b_tmpm", [128, NB, 256], F32)
            TMP2 = sb(st4, "b_tmp2", [128, 2, 2, 256], F32)
            SGT = sb(st4, "b_sgt", [128, 2, NB, 256], BF16)
            MB = sb(st4, "b_mb", [128, NB, 256], BF16)
            t_YBT, t_MT, t_TMPM, t_MB = Tk(), Tk(), Tk(), Tk()
            t_TMP2 = [Tk(), Tk()]
            t_SGT = [Tk(), Tk()]
            t_XT = [Tk(), Tk()]
            t_OT = [Tk(), Tk()]
            dma("sp", d_misc, YBT[:], ybT_d[tt].rearrange("k p t -> p k t"), reads=[t_ybT[tt]], writes=[t_YBT])
            for nt in range(16):
                c0 = nt * 256
                dma("sp", d_ld[0], SGT[:, 0], sga_d[tt, :, :, c0:c0 + 256].rearrange("b p c -> p b c"), reads=[t_sga[tt]], writes=[t_SGT[0]])
                dma("sp", d_ld[1], SGT[:, 1], sgb_d[tt, :, :, c0:c0 + 256].rearrange("b p c -> p b c"), reads=[t_sgb[tt]], writes=[t_SGT[1]])
                gemm_ntile(YAT, t_YAT, 16, w_a, sp_wa, nt)
                for b in range(4):
                    op("dve", lambda e, b=b: e.tensor_tensor(out=TMPM[:, 2 * b:2 * b + 2, :], in0=bankv(b, 256), in1=SGT[:, 0, 2 * b:2 * b + 2, :],
                                                             op=ALU.mult), reads=[t_PG[b], t_SGT[0]], writes=[t_TMPM])
                gemm_ntile(YBT, t_YBT, 16, w_b, sp_wb, nt)
                for b in range(4):
                    op("dve", lambda e, b=b: e.tensor_tensor(out=TMP2[:, b % 2], in0=bankv(b, 256), in1=SGT[:, 1, 2 * b:2 * b + 2, :],
                                                             op=ALU.mult), reads=[t_PG[b], t_SGT[1]], writes=[t_TMP2[b % 2]])
                    op("dve", lambda e, b=b: e.tensor_tensor(out=MB[:, 2 * b:2 * b + 2, :], in0=TMP2[:, b % 2], in1=TMPM[:, 2 * b:2 * b + 2, :],
                                                             op=ALU.add), reads=[t_TMP2[b % 2], t_TMPM], writes=[t_MB])
                for hh in range(2):
                    transposes_bf16([MB[:, blk, hh * 128:(hh + 1) * 128] for blk in range(8)], t_MB, hh)
                    op("act", lambda e, hh=hh, nt=nt: e.activation(out=MT[:, 2 * nt + hh, :], in_=PTb[:, hh * 1024:(hh + 1) * 1024], func=AF.Copy),
                       reads=[t_PT[hh]], writes=[t_MT])
            sc.barrier()
            st4.close()
            XT = sb(st, "b_xt", [128, 2, NB, 256], F32)
            OT = sb(st, "b_ot", [128, 2, NB, 256], F32)
            for nt in range(16):
                c0 = nt * 256
                s = nt % 2
                dma("sp", d_ld[2 + s], XT[:, s], x_own[tt * 8:(tt + 1) * 8, :, c0:c0 + 256].rearrange("b p c -> p b c"),
                    reads=[t_xown], writes=[t_XT[s]])
                gemm_ntile(MT, t_MT, 32, w_o, sp_wo, nt)
                for b in range(4):
                    op("dve", lambda e, b=b, s=s: e.tensor_tensor(out=OT[:, s, 2 * b:2 * b + 2, :], in0=bankv(b, 256), in1=XT[:, s, 2 * b:2 * b + 2, :],
                                                                  op=ALU.add), reads=[t_PG[b], t_XT[s]], writes=[t_OT[s]])
                dma("sp", d_st[s], x1_d[tt * 8:(tt + 1) * 8, :, c0:c0 + 256].rearrange("b p c -> p b c"), OT[:, s],
                    reads=[t_OT[s]], writes=[t_x1[tt][nt]])
            sc.barrier()
        stt.close()

    def s6(tt, pump=None):
        with ExitStack() as st:
            H2T = sb(st, "f_h2t", [128, 32, TT], BF16)
            t_H2T = Tk()
            norm_transpose(st, [(x1_d[tt * 8 + bi], t_dummy) for bi in range(8)], 1, H2T, t_H2T)
            AM = sb(st, "f_am", [128, 2, NB, 128], BF16)
            SL = sb(st, "f_sl", [128, 2, 2, 128], F32)
            AT = sb(st, "f_at", [128, 2, TT], BF16)
            t_AM = [Tk(), Tk()]
            t_SL = [Tk(), Tk()]
            t_AT = [Tk(), Tk()]
            for nt in range(86):
                if pump is not None and nt > 0:
                    pump()
                s = nt % 2
                gemm_ntile(H2T, t_H2T, 32, w_gu, sp_gu, nt)
                for b in range(4):
                    pv = bankv(b, 256)
                    op("act", lambda e, b=b, pv=pv: e.activation(out=SL[:, b % 2], in_=pv[:, :, 0:128], func=AF.Silu),
                       reads=[t_PG[b]], writes=[t_SL[b % 2]])
                    op("dve", lambda e, b=b, pv=pv, s=s: e.tensor_tensor(out=AM[:, s, 2 * b:2 * b + 2, :], in0=SL[:, b % 2], in1=pv[:, :, 128:256],
                                                                         op=ALU.mult), reads=[t_SL[b % 2], t_PG[b]], writes=[t_AM[s]])
                transposes_bf16([AM[:, s, blk, :] for blk in range(8)], t_AM[s], s)
                op("act", lambda e, s=s: e.activation(out=AT[:, s, :], in_=PTb[:, s * 1024:(s + 1) * 1024], func=AF.Copy),
                   reads=[t_PT[s]], writes=[t_AT[s]])
                dma("sp", d_st[s], actT_d[tt, nt], AT[:, s, :], reads=[t_AT[s]], writes=[t_actT[tt]])
            sc.barrier()
        with ExitStack() as st:
            ACTT = sb(st, "f_actt", [128, 43, TT], BF16)
            XT = sb(st, "f_xt", [128, 2, NB, 256], F32)
            OT = sb(st, "f_ot", [128, 2, NB, 256], F32)
            t_ACTT = Tk()
            t_XT = [Tk(), Tk()]
            t_OT = [Tk(), Tk()]
            for half in range(2):
                dma("sp", d_misc, ACTT[:], actT_d[tt, half * 43:(half + 1) * 43].rearrange("k p t -> p k t"),
                    reads=[t_actT[tt]], writes=[t_ACTT])
                for nt in range(16):
                    c0 = nt * 256
                    s = nt % 2
                    srcd = x1_d if half == 0 else x2_d
                    tsrc = t_x1[tt][nt] if half == 0 else t_x2[tt][nt]
                    dma("sp", d_ld[2 + s], XT[:, s], srcd[tt * 8:(tt + 1) * 8, :, c0:c0 + 256].rearrange("b p c -> p b c"),
                        reads=[tsrc], writes=[t_XT[s]])
                    if pump is not None:
                        pump()
                    gemm_ntile(ACTT, t_ACTT, 43, w_dn[half], sp_dn, nt)
                    for b in range(4):
                        op("dve", lambda e, b=b, s=s: e.tensor_tensor(out=OT[:, s, 2 * b:2 * b + 2, :], in0=bankv(b, 256),
                                                                      in1=XT[:, s, 2 * b:2 * b + 2, :], op=ALU.add),
                           reads=[t_PG[b], t_XT[s]], writes=[t_OT[s]])
                    dma("sp", d_st[s], x2_d[tt * 8:(tt + 1) * 8, :, c0:c0 + 256].rearrange("b p c -> p b c"), OT[:, s],
                        reads=[t_OT[s]], writes=[t_x2[tt][nt]])
            sc.barrier()

    def s7(tt):
        with ExitStack() as st:
            H3T = sb(st, "p_h3t", [128, 32, TT], BF16)
            t_H3T = Tk()
            norm_transpose(st, [(x2_d[tt * 8 + bi], t_dummy) for bi in range(8)], 2, H3T, t_H3T)
            PTT = sb(st, "p_ptt", [128, 2, TT], BF16)
            PB = sb(st, "p_pb", [128, 2, DPLE], F32)
            PGR = sb(st, "p_pgr", [128, D], F32)
            XT = sb(st, "p_xt", [128, 2, NB, 256], F32)
            OT = sb(st, "p_ot", [128, 2, NB, 256], F32)
            SGM = sb(st, "p_sgm", [128, NB, 256], F32)
            T1 = sb(st, "p_t1", [128, NB, 256], F32)
            SQ2 = sb(st, "p_sq2", [128, 2, 256], F32)
            t_PTT, t_PGR, t_SGM, t_T1, t_SQ2 = Tk(), Tk(), Tk(), Tk(), Tk()
            t_PB = [Tk(), Tk()]
            t_XT = [Tk(), Tk()]
            t_OT = [Tk(), Tk()]
            dma("sp", d_misc, PGR[:], pgrep_d[:, :], writes=[t_PGR])
            for bi in range(8):
                s = bi % 2
                dma("sp", d_ld[s], PB[:, s, :], p_own[tt * 8 + bi], reads=[t_pown], writes=[t_PB[s]])
                for j in range(2):
                    op("pe", lambda e, j=j, s=s: e.transpose(PM[:, s * 512 + j * 128: s * 512 + (j + 1) * 128], PB[:, s, j * 128:(j + 1) * 128], IDF[:]),
                       reads=[t_PB[s], t_const], writes=[t_PM[s]])
                op("act", lambda e, s=s, bi=bi: e.activation(out=PTT[:, :, bi * 128:(bi + 1) * 128],
                                                             in_=PM[:, s * 512: s * 512 + 256].rearrange("p (j c) -> p j c", j=2), func=AF.Copy),
                   reads=[t_PM[s]], writes=[t_PTT])
            op("dve", lambda e: e.memset(SM[:, 40:48], 0.0), writes=[t_SM])
            for nt in range(16):
                gemm_ntile(PTT, t_PTT, 2, w_pl, sp_pl, nt)
                for b in range(4):
                    op("act", lambda e, b=b: e.activation(out=SQ2[:], in_=bankv(b, 256), func=AF.Square), reads=[t_PG[b]], writes=[t_SQ2])
                    op("dve", lambda e, b=b: e.tensor_reduce(out=SM[:, 48:50], in_=SQ2[:], axis=AX.X, op=ALU.add), reads=[t_SQ2, t_SM], writes=[t_SM])
                    op("dve", lambda e, b=b: e.tensor_tensor(out=SM[:, 40 + 2 * b:42 + 2 * b], in0=SM[:, 40 + 2 * b:42 + 2 * b], in1=SM[:, 48:50],
                                                             op=ALU.add), reads=[t_SM], writes=[t_SM])
            rstd_from_ss(SM[:, 40:48], 8, D)
            for nt in range(16):
                c0 = nt * 256
                s = nt % 2
                dma("sp", d_ld[2 + s], XT[:, s], x2_d[tt * 8:(tt + 1) * 8, :, c0:c0 + 256].rearrange("b p c -> p b c"),
                    reads=[t_x2[tt][nt]], writes=[t_XT[s]])
                gemm_ntile(H3T, t_H3T, 32, w_pg, sp_pg, nt)
                for b in range(4):
                    op("act", lambda e, b=b: e.activation(out=SGM[:, 2 * b:2 * b + 2, :], in_=bankv(b, 256), func=AF.Sigmoid),
                       reads=[t_PG[b]], writes=[t_SGM])
                gemm_ntile(PTT, t_PTT, 2, w_pl, sp_pl, nt)
                for blk in range(8):
                    b, sub = blk // 2, blk % 2
                    i_ = PG[:, b * 512 + sub * 256: b * 512 + (sub + 1) * 256]
                    op("dve", lambda e, blk=blk, i_=i_, c0=c0: e.scalar_tensor_tensor(
                        out=T1[:, blk, :], in0=i_, scalar=SM[:, 40 + blk:41 + blk], in1=PGR[:, c0:c0 + 256], op0=ALU.mult, op1=ALU.mult),
                       reads=[t_PG[b], t_SM, t_PGR], writes=[t_T1])
                op("dve", lambda e: e.tensor_tensor(out=T1[:], in0=T1[:], in1=SGM[:], op=ALU.mult), reads=[t_T1, t_SGM], writes=[t_T1])
                op("dve", lambda e, s=s: e.tensor_tensor(out=OT[:, s], in0=T1[:], in1=XT[:, s], op=ALU.add),
                   reads=[t_T1, t_XT[s]], writes=[t_OT[s]])
                dma("sp", d_st[s], out_d[tt * 8:(tt + 1) * 8, :, c0:c0 + 256].rearrange("b p c -> p b c"), OT[:, s],
                    reads=[t_OT[s]], writes=[t_out])
            sc.barrier()

    def with_topk(tt_topk, stage_fn, tt_stage, per):
        with ExitStack() as stp:
            SCO1 = sb(stp, "k_sco", [128, S], F32)
            NS1 = sb(stp, "k_ns", [128, S], BF16)
            M8 = sb(stp, "k_m8", [128, 256], F32)
            g = topk_gen(tt_topk, SCO1, NS1, M8)

            def pump():
                for _ in range(per):
                    next(g, None)

            stage_fn(tt_stage, pump)
            run(g)
            sc.barrier()

    s1(0)
    s3_idx(0)
    with_topk(0, s1, 1, 8)
    s3_attn(0)
    s45(0)
    s3_idx(1)
    with_topk(1, s6, 0, 5)
    s7(0)
    s3_attn(1)
    s45(1)
    s6(1)
    s7(1)
    sc.barrier()
    return nc, sc, es


def t5_bucket_np(n):
    n = np.maximum(n, 0)
    nf = np.maximum(n, 1).astype(np.float32)
    large = 16 + (np.log(nf / 16) / math.log(128 / 16) * 16).astype(np.int32)
    large = np.minimum(large, 31)
    return np.where(n < 16, n, large)


def host_inputs(inp):
    f = lambda a: np.ascontiguousarray(np.asarray(a, dtype=np.float32))
    x = f(inp["x"]); p = f(inp["p"])[0]
    w_in = f(inp["w_in"])[0]
    offs = np.cumsum([0, 2048, 128, 128, 4096, 128, 32, 4096, 4096, 4096])
    q_, k_, v_, qi_, ki_, wi_, uv_, ga_, gb_ = [w_in[:, offs[i]:offs[i + 1]] for i in range(9)]
    w_own = np.concatenate([q_, qi_, wi_, uv_[:, GW:], uv_[:, :GW], ga_, gb_], axis=1)
    shared = {}
    shared["w_inp"] = pack_w(w_own, W_INP, KS4096)
    shared["w_kvi"] = pack_w(np.concatenate([k_, v_, ki_], axis=1), W_KVI, KS4096)
    shared["w_a"] = pack_w(f(inp["w_branch_a"])[0], W_4096, KS2048)
    shared["w_b"] = pack_w(f(inp["w_branch_b"])[0], W_4096, KS2048)
    shared["w_o"] = pack_w(f(inp["w_out"])[0], W_4096, KS4096)
    wg = f(inp["w_gate_ffn"])[0].reshape(D, 86, 1, 128); wu = f(inp["w_up_ffn"])[0].reshape(D, 86, 1, 128)
    shared["w_gu"] = pack_w(np.concatenate([wg, wu], axis=2).reshape(D, 2 * DFF), W_GU, KS4096)
    wd = f(inp["w_down_ffn"])[0]
    shared["w_dn"] = np.stack([pack_w(wd[:5504], W_4096, KS5504), pack_w(wd[5504:], W_4096, KS5504)])
    shared["w_pg"] = pack_w(f(inp["w_ple_gate"])[0], W_4096, KS4096)
    shared["w_pl"] = pack_w(f(inp["w_ple"])[0], W_4096, KS256)
    g3 = np.stack([f(inp["norm_mix_g"])[0], f(inp["norm_ffn_g"])[0], f(inp["norm_ple_g"])[0]])
    shared["gcols"] = np.ascontiguousarray(g3.reshape(3, 32, 128).transpose(2, 0, 1))
    shared["gqk"] = np.ascontiguousarray(np.broadcast_to(np.stack([f(inp["q_norm_g"])[0], f(inp["k_norm_g"])[0]])[None], (128, 2, 128)))
    shared["lnrep"] = np.ascontiguousarray(np.broadcast_to(np.stack([f(inp["sgu_ln_g"])[0], f(inp["sgu_ln_b"])[0]])[None], (128, 2, GW)))
    shared["pgrep"] = np.ascontiguousarray(np.broadcast_to(f(inp["ple_norm_g"])[0][None], (128, D)))
    shared["wsT"] = np.ascontiguousarray(f(inp["sgu_w"])[0].transpose(2, 0, 1))
    shared["trilT"] = np.ascontiguousarray(np.triu(np.ones((128, 128), np.float32)))
    shared["bs"] = np.ascontiguousarray(f(inp["sgu_b"])[0].T)
    shared["relb"] = f(inp["rel_bias"])
    shared["ident"] = np.ascontiguousarray(np.tile(np.eye(128, dtype=np.float32), (1, 4)))
    shared["anti"] = np.ascontiguousarray(np.fliplr(np.eye(128, dtype=np.float32)))
    maps = []
    for c in range(8):
        b, r = c // 2, c % 2
        m = dict(shared)
        xb = x[b].reshape(32, 128, D)
        m["x_seq"] = xb
        m["x_own"] = np.ascontiguousarray(xb[r::2])
        m["p_own"] = np.ascontiguousarray(p[b].reshape(32, 128, DPLE)[r::2])
        npr = np.arange(512)
        n = npr - 127 + (r - 1) * 128
        em = np.zeros((33, 512), np.float32)
        bk = t5_bucket_np(n)
        em[bk, npr] += 1.0
        em[31, :] -= 1.0
        em[:32, n < 0] = 0.0
        em[32, n < 0] = 1.0
        m["emat"] = em
        qpos = r * 128 + np.arange(128)[:, None]
        kpos = np.arange(256)[None, :]
        m["cmask"] = np.where(kpos <= qpos, 0.0, -1e30).astype(np.float32)
        maps.append(m)
    return maps


_CACHE = {}


def kernel(**inputs):
    maps = host_inputs(inputs)
    if "nc" not in _CACHE:
        _CACHE["nc"] = build()[0]
    res = run_bass_kernel_spmd(_CACHE["nc"], maps, core_ids=list(range(8)))
    out = np.empty((4, 32, 128, D), np.float32)
    for c in range(8):
        b, r = c // 2, c % 2
        out[b, r::2] = res.results[c]["out"]
    return out.reshape(4, S, D)
```
